# Optimizing a Trainium2 kernel written in Bass

```python
import jax, jax.numpy as jnp
from jax import lax
import numpy as np

D_MODEL = 1024
BATCH = 8
SEQ = 2048
DEPTH = 4

HEAD_DIM = 64
N_HEADS_TOTAL = D_MODEL // HEAD_DIM
N_HEADS_M = 4
N_HEADS_A = (N_HEADS_TOTAL - N_HEADS_M) // 2
N_HEADS_B = N_HEADS_TOTAL - N_HEADS_M - N_HEADS_A
WIDTH_A = N_HEADS_A * HEAD_DIM
WIDTH_B = N_HEADS_B * HEAD_DIM
WIDTH_M = N_HEADS_M * HEAD_DIM
MIX_WIDTH = WIDTH_A + WIDTH_B + WIDTH_M
IDX_HEADS = 4
IDX_DIM = 64
ROT_DIM = HEAD_DIM // 4
ROPE_THETA = 500000.0
CHUNK = 64
PREV_CHUNKS = 8
BAND_CHUNKS = PREV_CHUNKS + 1
REL_CLIP = 256
N_MEM = 256
TOPK_MAX = 256
Q_BLOCK = 128
D_FF = -(-8 * D_MODEL // (3 * 256)) * 256
IN_SIZES = (WIDTH_A, WIDTH_A, WIDTH_A,
            IDX_HEADS * IDX_DIM, IDX_DIM, IDX_HEADS,
            WIDTH_B, WIDTH_B, WIDTH_B,
            WIDTH_M)
D_IN = sum(IN_SIZES)
EPS = 1e-6

kernel_name = "hybrid_dsa_chunkband_memory_block"


def rms_norm(x, g):
    xf = x.astype(jnp.float32)
    y = xf * lax.rsqrt(jnp.mean(xf * xf, axis=-1, keepdims=True) + EPS)
    return (y * g.astype(jnp.float32)).astype(x.dtype)


def partial_rope(x, positions):
    half = ROT_DIM // 2
    inv_freq = jnp.power(ROPE_THETA, -jnp.arange(half, dtype=jnp.float32) / half)
    ang = positions.astype(jnp.float32)[..., None] * inv_freq
    cos = jnp.cos(ang)[:, :, None, :]
    sin = jnp.sin(ang)[:, :, None, :]
    xf = x.astype(jnp.float32)
    x1 = xf[..., :half]
    x2 = xf[..., half:ROT_DIM]
    out = jnp.concatenate([x1 * cos - x2 * sin, x2 * cos + x1 * sin, xf[..., ROT_DIM:]], axis=-1)
    return out.astype(x.dtype)


def dsa_sparse_attention(q, k, v, q_idx, k_idx, w_idx):
    B, S = q.shape[0], q.shape[1]
    topk = min(TOPK_MAX, S // 4)
    nqb = S // Q_BLOCK
    key_chunk = jnp.arange(S) // CHUNK
    k_idx_f = k_idx.astype(jnp.float32)
    starts = jnp.arange(nqb) * Q_BLOCK

    def to_blocks(t):
        return t.reshape((B, nqb, Q_BLOCK) + t.shape[2:]).swapaxes(0, 1)

    def one_block(args):
        qb, qib, wb, start = args
        q_chunk = (start + jnp.arange(Q_BLOCK)) // CHUNK
        allowed = key_chunk[None, :] <= q_chunk[:, None]
        dots = jnp.einsum('bqhd,bsd->bqhs', qib.astype(jnp.float32), k_idx_f) * (IDX_DIM ** -0.5)
        score = jnp.einsum('bqh,bqhs->bqs', wb.astype(jnp.float32) * (IDX_HEADS ** -0.5), jax.nn.relu(dots))
        score = jnp.where(allowed[None], score, -jnp.inf)
        _, sel = lax.top_k(score, topk)
        valid = key_chunk[sel] <= q_chunk[None, :, None]
        k_sel = jax.vmap(lambda kk, ii: kk[ii])(k, sel)
        v_sel = jax.vmap(lambda vv, ii: vv[ii])(v, sel)
        logits = jnp.einsum('bqhd,bqkhd->bhqk', qb, k_sel).astype(jnp.float32) * (HEAD_DIM ** -0.5)
        logits = jnp.where(valid[:, None], logits, -jnp.inf)
        p = jax.nn.softmax(logits, axis=-1).astype(v.dtype)
        return jnp.einsum('bhqk,bqkhd->bqhd', p, v_sel)

    out = lax.map(one_block, (to_blocks(q), to_blocks(q_idx), to_blocks(w_idx), starts))
    return out.swapaxes(0, 1).reshape(B, S, -1)


def chunked_relbias_attention(q, k, v, rel_bias):
    B, S, H, Dh = q.shape
    nc = S // CHUNK
    qc = q.reshape(B, nc, CHUNK, H, Dh)

    def band(t):
        tc = t.reshape(B, nc, CHUNK, H, Dh)
        tp = jnp.pad(tc, ((0, 0), (PREV_CHUNKS, 0), (0, 0), (0, 0), (0, 0)))
        return jnp.concatenate([tp[:, j:j + nc] for j in range(BAND_CHUNKS)], axis=2)

    kb, vb = band(k), band(v)
    i = jnp.arange(CHUNK)
    m = jnp.arange(BAND_CHUNKS * CHUNK)
    rel = PREV_CHUNKS * CHUNK + i[:, None] - m[None, :]
    bias = rel_bias[:, jnp.clip(rel, -REL_CLIP, REL_CLIP) + REL_CLIP].astype(jnp.float32)
    key_chunk = jnp.arange(nc)[:, None] - PREV_CHUNKS + m[None, :] // CHUNK
    valid = key_chunk >= 0
    logits = jnp.einsum('bnqhd,bnkhd->bnhqk', qc, kb).astype(jnp.float32) * (HEAD_DIM ** -0.5) + bias[None, None]
    logits = jnp.where(valid[None, :, None, None, :], logits, -jnp.inf)
    p = jax.nn.softmax(logits, axis=-1).astype(v.dtype)
    out = jnp.einsum('bnhqk,bnkhd->bnqhd', p, vb)
    return out.reshape(B, S, H * Dh)


def memory_cross_attention(q, mem_k, mem_v):
    B, S = q.shape[0], q.shape[1]
    logits = jnp.einsum('bshd,bnhd->bhsn', q, mem_k).astype(jnp.float32) * (HEAD_DIM ** -0.5)
    p = jax.nn.softmax(logits, axis=-1).astype(mem_v.dtype)
    return jnp.einsum('bhsn,bnhd->bshd', p, mem_v).reshape(B, S, -1)


def setup_inputs(seed: int = 0) -> dict:
    key = jax.random.key(seed)
    ks = jax.random.split(key, 20)
    f32 = jnp.float32

    def nrm(k, shape, scale):
        return jax.random.normal(k, shape, f32) * scale

    def gain(k, shape):
        return 1.0 + 0.02 * jax.random.normal(k, shape, f32)

    x = jax.random.normal(ks[0], (BATCH, SEQ, D_MODEL), f32)
    mem = jax.random.normal(ks[1], (BATCH, N_MEM, D_MODEL), f32)
    offsets = jax.random.randint(ks[2], (BATCH, 1), 0, 64) * CHUNK
    positions = (offsets + jnp.arange(SEQ)[None, :]).astype(jnp.int32)
    return {
        "x": x,
        "mem": mem,
        "positions": positions,
        "g_mix": gain(ks[3], (DEPTH, D_MODEL)),
        "w_in": nrm(ks[4], (DEPTH, D_MODEL, D_IN), D_MODEL ** -0.5),
        "g_q_a": gain(ks[5], (DEPTH, HEAD_DIM)),
        "g_k_a": gain(ks[6], (DEPTH, HEAD_DIM)),
        "g_k_idx": gain(ks[7], (DEPTH, IDX_DIM)),
        "g_q_b": gain(ks[8], (DEPTH, HEAD_DIM)),
        "g_k_b": gain(ks[9], (DEPTH, HEAD_DIM)),
        "rel_bias": nrm(ks[10], (DEPTH, N_HEADS_B, 2 * REL_CLIP + 1), 0.1),
        "g_q_m": gain(ks[11], (DEPTH, HEAD_DIM)),
        "g_k_m": gain(ks[12], (DEPTH, HEAD_DIM)),
        "g_mem": gain(ks[13], (DEPTH, D_MODEL)),
        "w_mem_kv": nrm(ks[14], (DEPTH, D_MODEL, 2 * WIDTH_M), D_MODEL ** -0.5),
        "w_out": nrm(ks[15], (DEPTH, MIX_WIDTH, D_MODEL), 0.5 * MIX_WIDTH ** -0.5),
        "g_ffn": gain(ks[16], (DEPTH, D_MODEL)),
        "w_gate_up": nrm(ks[17], (DEPTH, D_MODEL, 2 * D_FF), D_MODEL ** -0.5),
        "w_down": nrm(ks[18], (DEPTH, D_FF, D_MODEL), 0.5 * D_FF ** -0.5),
    }


def reference(x, mem, positions, g_mix, w_in, g_q_a, g_k_a, g_k_idx, g_q_b, g_k_b, rel_bias,
              g_q_m, g_k_m, g_mem, w_mem_kv, w_out, g_ffn, w_gate_up, w_down):
    B, S = x.shape[0], x.shape[1]
    n_mem = mem.shape[1]
    split_points = np.cumsum(IN_SIZES)[:-1].tolist()

    def heads(t, n):
        return t.reshape(B, S, n, HEAD_DIM)

    for l in range(DEPTH):
        h = rms_norm(x, g_mix[l])
        proj = h @ w_in[l]
        qa, ka, va, qi, ki, wi, qb, kb, vb, qm = jnp.split(proj, split_points, axis=-1)

        qa = partial_rope(rms_norm(heads(qa, N_HEADS_A), g_q_a[l]), positions)
        ka = partial_rope(rms_norm(heads(ka, N_HEADS_A), g_k_a[l]), positions)
        va = heads(va, N_HEADS_A)
        qi = partial_rope(qi.reshape(B, S, IDX_HEADS, IDX_DIM), positions)
        ki = partial_rope(rms_norm(ki, g_k_idx[l])[:, :, None, :], positions)[:, :, 0]
        out_a = dsa_sparse_attention(qa, ka, va, qi, ki, wi)

        qb = rms_norm(heads(qb, N_HEADS_B), g_q_b[l])
        kb = rms_norm(heads(kb, N_HEADS_B), g_k_b[l])
        vb = heads(vb, N_HEADS_B)
        out_b = chunked_relbias_attention(qb, kb, vb, rel_bias[l])

        qm = rms_norm(heads(qm, N_HEADS_M), g_q_m[l])
        mkv = rms_norm(mem, g_mem[l]) @ w_mem_kv[l]
        mk, mv = jnp.split(mkv, 2, axis=-1)
        mk = rms_norm(mk.reshape(B, n_mem, N_HEADS_M, HEAD_DIM), g_k_m[l])
        mv = mv.reshape(B, n_mem, N_HEADS_M, HEAD_DIM)
        out_m = memory_cross_attention(qm, mk, mv)

        x = x + jnp.concatenate([out_a, out_b, out_m], axis=-1) @ w_out[l]

        h = rms_norm(x, g_ffn[l])
        gate, up = jnp.split(h @ w_gate_up[l], 2, axis=-1)
        x = x + (jax.nn.silu(gate) * up) @ w_down[l]
    return x
```

```python
import math
import os
from contextlib import ExitStack

import numpy as np
import concourse.bass as bass
import concourse.mybir as mybir
from concourse.bass_utils import run_bass_kernel_spmd

F32 = mybir.dt.float32
BF16 = mybir.dt.bfloat16
I32 = mybir.dt.int32
ALU = mybir.AluOpType
AF = mybir.ActivationFunctionType
AX = mybir.AxisListType

P = 128
SEQ = 2048
NT = 16
D = 1024
DC = 8
DFF = 2816
FB = 22
DEPTH = 4
D_IN = 2884
EPS = 1e-6
NBIS = 18
TOPK = 256
NEG = -1.0e30
ENGS = ('pe', 'act', 'dve', 'pool', 'sp')


class Sync:
    def __init__(self, nc, es, nslots=8):
        self.nc = nc
        self.eng = {'pe': nc.tensor, 'act': nc.scalar, 'dve': nc.vector, 'pool': nc.gpsimd, 'sp': nc.sync}
        self.sem = {}
        self.cnt = {}
        for k in ENGS:
            self.sem[k] = es.enter_context(nc.semaphore("s_" + k))
            self.cnt[k] = 0
        self.slots = {}
        for q in ('sp', 'pool'):
            self.slots[q] = []
            for i in range(nslots):
                nm = "d%s%d" % (q, i)
                self.sem[nm] = es.enter_context(nc.semaphore(nm))
                self.cnt[nm] = 0
                self.slots[q].append(nm)
        self.slot_next = {'sp': 0, 'pool': 0}
        self.seen = {k: {} for k in ENGS}
        self.writer = {}
        self.readers = {}
        self.n_ins = 0
        self.n_wait = 0

    def _need(self, e, deps):
        best = {}
        for (x, v) in deps:
            if v > best.get(x, 0):
                best[x] = v
        for x, v in best.items():
            if self.seen[e].get(x, 0) >= v:
                continue
            self.eng[e].wait_ge(self.sem[x], v)
            self.seen[e][x] = v
            self.n_wait += 1

    def _deps(self, e, reads, writes):
        deps = []
        for k in reads:
            w = self.writer.get(k)
            if w is not None:
                deps.append(w)
        for k in writes:
            w = self.writer.get(k)
            if w is not None and w[0] != e:
                deps.append(w)
            for (x, v) in self.readers.get(k, {}).items():
                if x != e:
                    deps.append((x, v))
        return deps

    @staticmethod
    def _is_psum(k):
        return (isinstance(k, tuple) and k[0] in ('pP', 'pS', 'psT')) or k in ('pO', 'pY')

    def op(self, e, fn, reads=(), writes=(), inc=True):
        writes = list(writes) + [k for k in reads if self._is_psum(k) and k not in writes]
        self._need(e, self._deps(e, reads, writes))
        ins = fn(self.eng[e])
        self.n_ins += 1
        if inc:
            self.cnt[e] += 1
            ins.then_inc(self.sem[e], 1)
            c = self.cnt[e]
        else:
            c = self.cnt[e] + 1
        for k in reads:
            self.readers.setdefault(k, {})[e] = c
        for k in writes:
            self.writer[k] = (e, c)
            self.readers[k] = {}
        return ins

    def dma(self, q, out, in_, reads=(), writes=(), **kw):
        slot = self.slots[q][self.slot_next[q] % len(self.slots[q])]
        self.slot_next[q] += 1
        deps = self._deps(None, reads, writes)
        deps.append((slot, self.cnt[slot]))
        self._need(q, deps)
        ins = self.eng[q].dma_start(out=out, in_=in_, **kw)
        self.n_ins += 1
        self.cnt[slot] += 16
        ins.then_inc(self.sem[slot], 16)
        c = self.cnt[slot]
        for k in reads:
            self.readers.setdefault(k, {})[slot] = c
        for k in writes:
            self.writer[k] = (slot, c)
            self.readers[k] = {}
        return ins

    def barrier(self):
        for e in ENGS:
            self._need(e, [(x, self.cnt[x]) for x in ENGS if x != e])

    def drain_dma(self, e):
        self._need(e, [(s, self.cnt[s]) for q in self.slots for s in self.slots[q]])


class _Stop(Exception):
    pass


def build(depth=DEPTH, stop=None):
    nc = bass.Bass("TRN2", target_bir_lowering=False)

    def din(name, shape, dt=F32):
        return nc.dram_tensor(name, list(shape), dt, kind="ExternalInput").ap()

    x_d = din("x", [SEQ, D])
    mem_d = din("mem", [256, D])
    pos_d = din("pos", [P, NT], I32)
    g_mix_d = din("g_mix", [DEPTH, D])
    g_ffn_d = din("g_ffn", [DEPTH, D])
    g_mem_d = din("g_mem", [DEPTH, D])
    hg_d = din("hgains", [DEPTH, 7, 64])
    w_in_d = din("w_in", [DEPTH, D, D_IN])
    w_mkv_d = din("w_mem_kv", [DEPTH, D, 512])
    w_out_d = din("w_out", [DEPTH, D, D])
    w_gu_d = din("w_gate_up", [DEPTH, D, 2 * DFF])
    w_dn_d = din("w_down", [DEPTH, DFF, D])
    bias_d = din("biasT", [DEPTH, P, 6, 5, P])
    out_d = nc.dram_tensor("out", [SEQ, D], F32, kind="ExternalOutput").ap()

    with ExitStack() as es:
        S = Sync(nc, es)

        def finish():
            for t in range(NT):
                S.dma('sp', out=out_d[t * P:(t + 1) * P, :], in_=x_sb[:, t, :], reads=[('x', t)])
            S.drain_dma('sp')
            S.barrier()
            print("kernel build: instructions=%d waits=%d sbuf_left=%d" % (S.n_ins, S.n_wait, nc.sbuf_bytes_remaining))

        def chk(name):
            if stop == name:
                S.barrier()
                finish()
                _Stop.nc = nc
                raise _Stop()

        uniq = [0]

        def sb(name, shape, dt, stack=es):
            uniq[0] += 1
            return stack.enter_context(nc.sbuf_tensor("%s_%d" % (name, uniq[0]), list(shape), dt))

        def ps(name, shape, dt):
            return es.enter_context(nc.psum_tensor(name, list(shape), dt))

        x_sb = sb("x_sb", [P, NT, D], F32)
        ident = sb("ident", [P, P], BF16)
        identf = sb("identf", [P, P], F32)
        cs_all = sb("cs_all", [P, NT, 16], F32)
        sn_all = sb("sn_all", [P, NT, 16], F32)
        cneg = sb("cneg", [P, 32], F32)
        pw = sb("pw", [P, NBIS + 1], F32)
        ss = sb("ss", [P, NT], F32)
        rs1 = sb("rs1", [P, NT], F32)
        rstd = sb("rstd", [P, NT], F32)
        rstd_mem = sb("rstd_mem", [P, 2], F32)
        small = sb("small", [P, 64], F32)
        gbc = sb("gbc", [P, D], F32)
        hg = [sb("hg%d" % i, [P, 7, 64], F32) for i in range(2)]
        hb = [sb("hb%d" % i, [P, D], BF16) for i in range(2)]
        sqj = sb("sqj", [P, D], BF16)

        pP = [ps("pP%d" % i, [P, 512], F32) for i in range(2)]
        pS = [ps("pS%d" % i, [P, 512], F32) for i in range(2)]
        pO = ps("pO", [P, 512], F32)
        pY = ps("pY", [P, 512], F32)
        psT = [ps("psT%d" % i, [P, 1024], BF16) for i in range(2)]
        psT_ctr = [0]

        def next_psT():
            i = psT_ctr[0] % 2
            psT_ctr[0] += 1
            return psT[i], ('psT', i)

        for t in range(NT):
            S.dma('sp', out=x_sb[:, t, :], in_=x_d[t * P:(t + 1) * P, :], writes=[('x', t)])

        S.op('pool', lambda e: e.memset(identf[:], 0.0), writes=['identf'])
        S.op('pool', lambda e: e.affine_select(out=identf[:], in_=identf[:], pattern=[[-1, P]],
                                               compare_op=ALU.not_equal, fill=1.0, base=0,
                                               channel_multiplier=1), reads=['identf'], writes=['identf'])
        S.op('pool', lambda e: e.tensor_copy(out=ident[:], in_=identf[:]), reads=['identf'], writes=['ident'])
        S.op('pool', lambda e: e.memset(cneg[:], -0.5), writes=['cneg'])
        for k in range(NBIS + 1):
            S.op('pool', lambda e, k=k: e.memset(pw[:, k:k + 1], 2.0 ** (-k)), writes=['pw'])

        with ExitStack() as st:
            pos_i = sb("pos_i", [P, NT], I32, st)
            posf = sb("posf", [P, NT], F32, st)
            invf = sb("invf", [P, 8], F32, st)
            ang = sb("ang", [P, NT, 8], F32, st)
            a2 = sb("a2", [P, NT, 8], F32, st)
            kf = sb("kf", [P, NT, 8], F32, st)
            ki_ = sb("ki_", [P, NT, 8], I32, st)
            m1 = sb("m1", [P, NT, 8], F32, st)
            sv = sb("sv", [P, NT, 8], F32, st)
            S.dma('sp', out=pos_i[:], in_=pos_d[:, :], writes=['pos_i'])
            S.op('dve', lambda e: e.tensor_copy(out=posf[:], in_=pos_i[:]), reads=['pos_i'], writes=['posf'])
            for i in range(8):
                S.op('dve', lambda e, i=i: e.memset(invf[:, i:i + 1], float(500000.0 ** (-i / 8.0))), writes=['invf'])
            S.op('dve', lambda e: e.tensor_tensor(out=ang[:], in0=posf[:].unsqueeze(2).to_broadcast([P, NT, 8]),
                                                  in1=invf[:].unsqueeze(1).to_broadcast([P, NT, 8]), op=ALU.mult),
                 reads=['posf', 'invf'], writes=['ang'])
            TWO_PI = 2.0 * math.pi
            C1 = 6.28125
            C2 = TWO_PI - C1
            for which, off in (('sin', 0.0), ('cos', math.pi / 2.0)):
                S.op('dve', lambda e: e.tensor_scalar(out=a2[:], in0=ang[:], scalar1=off, scalar2=None, op0=ALU.add),
                     reads=['ang'], writes=['a2'])
                S.op('dve', lambda e: e.tensor_scalar(out=kf[:], in0=a2[:], scalar1=1.0 / TWO_PI, scalar2=None, op0=ALU.mult),
                     reads=['a2'], writes=['kf'])
                S.op('dve', lambda e: e.tensor_copy(out=ki_[:], in_=kf[:]), reads=['kf'], writes=['ki_'])
                S.op('dve', lambda e: e.tensor_copy(out=kf[:], in_=ki_[:]), reads=['ki_'], writes=['kf'])
                S.op('dve', lambda e: e.scalar_tensor_tensor(out=a2[:], in0=kf[:], scalar=-C1, in1=a2[:], op0=ALU.mult, op1=ALU.add),
                     reads=['kf', 'a2'], writes=['a2'])
                S.op('dve', lambda e: e.scalar_tensor_tensor(out=a2[:], in0=kf[:], scalar=-C2, in1=a2[:], op0=ALU.mult, op1=ALU.add),
                     reads=['kf', 'a2'], writes=['a2'])
                S.op('dve', lambda e: e.tensor_scalar(out=m1[:], in0=a2[:], scalar1=math.pi, scalar2=-TWO_PI, op0=ALU.is_gt, op1=ALU.mult),
                     reads=['a2'], writes=['m1'])
                S.op('dve', lambda e: e.tensor_tensor(out=a2[:], in0=a2[:], in1=m1[:], op=ALU.add), reads=['a2', 'm1'], writes=['a2'])
                S.op('dve', lambda e: e.tensor_scalar(out=m1[:], in0=a2[:], scalar1=-math.pi, scalar2=TWO_PI, op0=ALU.is_lt, op1=ALU.mult),
                     reads=['a2'], writes=['m1'])
                S.op('dve', lambda e: e.tensor_tensor(out=a2[:], in0=a2[:], in1=m1[:], op=ALU.add), reads=['a2', 'm1'], writes=['a2'])
                S.op('dve', lambda e: e.tensor_scalar(out=a2[:], in0=a2[:], scalar1=3.1415925, scalar2=-3.1415925, op0=ALU.min, op1=ALU.max),
                     reads=['a2'], writes=['a2'])
                S.op('act', lambda e: e.activation(out=sv[:], in_=a2[:], func=AF.Sin), reads=['a2'], writes=['sv'])
                if which == 'sin':
                    S.op('dve', lambda e: e.tensor_scalar(out=sn_all[:, :, 0:8], in0=sv[:], scalar1=-1.0, scalar2=None, op0=ALU.mult),
                         reads=['sv'], writes=['sn_all'])
                    S.op('dve', lambda e: e.tensor_copy(out=sn_all[:, :, 8:16], in_=sv[:]), reads=['sv'], writes=['sn_all'])
                else:
                    S.op('dve', lambda e: e.tensor_copy(out=cs_all[:, :, 0:8], in_=sv[:]), reads=['sv'], writes=['cs_all'])
                    S.op('dve', lambda e: e.tensor_copy(out=cs_all[:, :, 8:16], in_=sv[:]), reads=['sv'], writes=['cs_all'])
            S.barrier()

        with ExitStack() as st:
            mem_sb = sb("mem_sb0", [P, 2, D], F32, st)
            for mt in range(2):
                S.dma('sp', out=mem_sb[:, mt, :], in_=mem_d[mt * P:(mt + 1) * P, :], writes=['mem_sb'])
            for mt in range(2):
                S.op('act', lambda e, mt=mt: e.activation(out=sqj[:], in_=mem_sb[:, mt, :], func=AF.Square,
                                                          accum_out=small[:, mt:mt + 1]),
                     reads=['mem_sb'], writes=['sqj', 'small'])
            S.op('dve', lambda e: e.tensor_scalar(out=small[:, 2:4], in0=small[:, 0:2], scalar1=1.0 / D, scalar2=EPS,
                                                  op0=ALU.mult, op1=ALU.add), reads=['small'], writes=['small'])
            S.op('pool', lambda e: e.tensor_tensor(out=rstd_mem[:], in0=small[:, 2:4], in1=cneg[:, 0:2], op=ALU.pow),
                 reads=['small', 'cneg'], writes=['rstd_mem'])
            S.barrier()

        chk('setup')
        def bcast_rows(src):
            n = 1
            for s_ in src.shape:
                n *= s_
            return bass.AP(tensor=src.tensor, offset=src.offset, ap=[[0, P], [1, n]])

        def load_gbc(src2d):
            S.dma('sp', out=gbc[:], in_=bcast_rows(src2d), writes=['gbc'])

        def norm_stats():
            for t in range(NT):
                S.op('act', lambda e, t=t: e.activation(out=sqj[:], in_=x_sb[:, t, :], func=AF.Square,
                                                        accum_out=ss[:, t:t + 1]),
                     reads=[('x', t)], writes=['sqj', 'ss'])
            S.op('dve', lambda e: e.tensor_scalar(out=rs1[:], in0=ss[:], scalar1=1.0 / D, scalar2=EPS, op0=ALU.mult, op1=ALU.add),
                 reads=['ss'], writes=['rs1'])
            S.op('pool', lambda e: e.tensor_tensor(out=rstd[:], in0=rs1[:], in1=cneg[:, 0:NT], op=ALU.pow),
                 reads=['rs1', 'cneg'], writes=['rstd'])

        def make_hT(t, dst, dst_key):
            b = t % 2
            S.op('dve', lambda e: e.scalar_tensor_tensor(out=hb[b][:], in0=x_sb[:, t, :], scalar=rstd[:, t:t + 1],
                                                         in1=gbc[:], op0=ALU.mult, op1=ALU.mult),
                 reads=[('x', t), 'rstd', 'gbc'], writes=[('hb', b)])
            pT, pk = next_psT()
            for c in range(DC):
                S.op('pe', lambda e, c=c: e.transpose(out=pT[:, c * P:(c + 1) * P], in_=hb[b][:, c * P:(c + 1) * P], identity=ident[:]),
                     reads=[('hb', b), 'ident'], writes=[pk], inc=(c == DC - 1))
            S.op('act', lambda e: e.activation(out=dst, in_=pT[:].rearrange("p (c k) -> p c k", c=DC), func=AF.Copy),
                 reads=[pk], writes=[dst_key])

        def w_view(wd, l):
            return wd[l].rearrange("(c p) n -> p c n", p=P)

        def load_w(dst, src, key):
            S.dma('pool', out=dst, in_=src, writes=[key], max_dma_last_dim=4096)

        def head_rstd(sq_ap, nh, ssh, rsh):
            S.op('dve', lambda e: e.tensor_reduce(out=ssh[:, 0:nh], in_=sq_ap.rearrange("p (h d) -> p h d", d=64), axis=AX.X, op=ALU.add),
                 reads=['sq'], writes=['ssh'])
            S.op('dve', lambda e: e.tensor_scalar(out=ssh[:, 16:16 + nh], in0=ssh[:, 0:nh], scalar1=1.0 / 64, scalar2=EPS,
                                                  op0=ALU.mult, op1=ALU.add), reads=['ssh'], writes=['ssh'])
            S.op('pool', lambda e: e.tensor_tensor(out=rsh[:, 0:nh], in0=ssh[:, 16:16 + nh], in1=cneg[:, 0:nh], op=ALU.pow),
                 reads=['ssh', 'cneg'], writes=['rsh'])

        def out_proj(j, catb, catT, ncat, Wo, c_base):
            pT, pk = next_psT()
            for c in range(ncat):
                S.op('pe', lambda e, c=c: e.transpose(out=pT[:, c * P:(c + 1) * P], in_=catb[:, c * P:(c + 1) * P], identity=ident[:]),
                     reads=['catb', 'ident'], writes=[pk], inc=(c == ncat - 1))
            S.op('act', lambda e: e.activation(out=catT[:, 0:ncat * P], in_=pT[:, 0:ncat * P], func=AF.Copy),
                 reads=[pk], writes=['catT'])
            for half in range(2):
                pb = pP[half]
                for c in range(ncat):
                    S.op('pe', lambda e, c=c: e.matmul(pb[:, :], lhsT=catT[:, c * P:(c + 1) * P],
                                                       rhs=Wo[:, c_base + c, half * 512:(half + 1) * 512],
                                                       start=(c == 0), stop=(c == ncat - 1)),
                         reads=['catT', 'Wo'], writes=[('pP', half)], inc=(c == ncat - 1))
                S.op('dve', lambda e: e.tensor_tensor(out=x_sb[:, j, half * 512:(half + 1) * 512],
                                                      in0=x_sb[:, j, half * 512:(half + 1) * 512], in1=pb[:, :], op=ALU.add),
                     reads=[('pP', half), ('x', j)], writes=[('x', j)])

        def normalize_cat(nh, catb, rden):
            pOv = pO[:, 0:nh * 65].rearrange("p (h d) -> p h d", d=65)
            S.op('dve', lambda e: e.reciprocal(out=rden[:, 0:nh], in_=pOv[:, :, 64]), reads=['pO'], writes=['rden'])
            S.op('dve', lambda e: e.tensor_tensor(out=catb[:, 0:nh * 64].rearrange("p (h d) -> p h d", d=64),
                                                  in0=pOv[:, :, 0:64],
                                                  in1=rden[:, 0:nh].unsqueeze(2).to_broadcast([P, nh, 64]), op=ALU.mult),
                 reads=['pO', 'rden'], writes=['catb'])

        for l in range(depth):
            hgl = hg[l % 2]
            hk = ('hg', l % 2)
            with ExitStack() as sa:
                W1 = sb("W1", [P, DC, 1476], BF16, sa)
                W2 = sb("W2", [P, DC, 1152], BF16, sa)
                Wo = sb("Wo", [P, 3, D], BF16, sa)
                kT = sb("kT", [P, 3, SEQ], BF16, sa)
                v_aug = sb("v_aug", [P, NT, 6, 65], BF16, sa)
                hTt = [sb("hTt%d" % i, [P, DC, P], BF16, sa) for i in range(2)]
                pj = sb("pj", [P, 1476], F32, sa)
                sq = sb("sq", [P, 832], F32, sa)
                qn = sb("qn", [P, 1088], F32, sa)
                qkb = sb("qkb", [P, 1088], BF16, sa)
                qT_t = [sb("qT_t%d" % i, [P, 3, P], BF16, sa) for i in range(2)]
                ssh = sb("ssh", [P, 32], F32, sa)
                rsh = sb("rsh", [P, 16], F32, sa)
                rden = sb("rden", [P, 8], F32, sa)
                catb = sb("catb", [P, 384], BF16, sa)
                catT = sb("catT", [P, 384], BF16, sa)
                Eb = [sb("Eb%d" % i, [P, 512], BF16, sa) for i in range(2)]
                PT = [sb("PT%d" % i, [P, 512], BF16, sa) for i in range(2)]

                S.dma('sp', out=hgl[:].rearrange("p a d -> p (a d)"),
                      in_=bcast_rows(hg_d[l:l + 1]), writes=[hk])
                wv = w_view(w_in_d, l)
                load_w(W1[:, :, 0:768], wv[:, :, 0:768], 'W1')
                load_w(W1[:, :, 768:1092], wv[:, :, 1152:1476], 'W1')
                load_w(W1[:, :, 1092:1476], wv[:, :, 768:1152], 'W1')
                wov = w_view(w_out_d, l)
                load_w(Wo[:, 0:3, :], wov[:, 0:3, :], 'Wo')
                load_w(W2[:, :, 0:576], wv[:, :, 1476:2052], 'W2')
                load_w(W2[:, :, 576:1152], wv[:, :, 2052:2628], 'W2')
                load_gbc(g_mix_d[l:l + 1, :])
                S.op('pool', lambda e: e.memset(v_aug[:], 1.0), writes=['v_all'])
                norm_stats()
                chk('norm')

                with ExitStack() as sg:
                    kiT2 = sb("kiT2", [P, SEQ], BF16, sg)
                    qiT_t = [sb("qiT_t%d" % i, [P, 2, P], BF16, sg) for i in range(2)]
                    sc = sb("sc", [P, SEQ], F32, sg)
                    Mk = sb("Mk", [P, SEQ], BF16, sg)
                    MT = sb("MT", [P, SEQ], BF16, sg)
                    rl = [sb("rl%d" % i, [P, 512], F32, sg) for i in range(2)]
                    t1 = sb("t1", [P, 17, 16], F32, sg)
                    t2 = sb("t2", [P, 17, 16], F32, sg)
                    ws_t = sb("ws_t", [P, 4], F32, sg)
                    bst = sb("bst", [P, 8], F32, sg)
                    Bk = sb("Bk", [P, NBIS + 1], F32, sg)
                    gi = 0
                    ri = 0
                    for t in range(NT):
                        b = t % 2
                        make_hT(t, hTt[b][:], ('hTt', b))
                        chk('A_h%d' % t)
                        for s_, (c0, c1) in enumerate(((0, 512), (512, 1024), (1024, 1476))):
                            pb = pP[s_ % 2]
                            for c in range(DC):
                                S.op('pe', lambda e, c=c: e.matmul(pb[:, 0:c1 - c0], lhsT=hTt[b][:, c, :], rhs=W1[:, c, c0:c1],
                                                                   start=(c == 0), stop=(c == DC - 1)),
                                     reads=[('hTt', b), 'W1'], writes=[('pP', s_ % 2)], inc=(c == DC - 1))
                            S.op('act', lambda e: e.activation(out=pj[:, c0:c1], in_=pb[:, 0:c1 - c0], func=AF.Copy),
                                 reads=[('pP', s_ % 2)], writes=['pj'])
                        chk('A_p%d' % t)
                        S.op('act', lambda e: e.activation(out=sq[:, 0:768], in_=pj[:, 0:768], func=AF.Square), reads=['pj'], writes=['sq'])
                        S.op('act', lambda e: e.activation(out=sq[:, 768:832], in_=pj[:, 1024:1088], func=AF.Square), reads=['pj'], writes=['sq'])
                        head_rstd(sq[:, 0:832], 13, ssh, rsh)
                        S.op('dve', lambda e: e.tensor_tensor(out=qn[:, 0:768].rearrange("p (h d) -> p h d", d=64),
                                                              in0=pj[:, 0:768].rearrange("p (h d) -> p h d", d=64),
                                                              in1=rsh[:, 0:12].unsqueeze(2).to_broadcast([P, 12, 64]), op=ALU.mult),
                             reads=['pj', 'rsh'], writes=['qn'])
                        S.op('dve', lambda e: e.tensor_scalar(out=qn[:, 1024:1088], in0=pj[:, 1024:1088], scalar1=rsh[:, 12:13], scalar2=None, op0=ALU.mult),
                             reads=['pj', 'rsh'], writes=['qn'])
                        S.op('dve', lambda e: e.tensor_tensor(out=qn[:, 0:384].rearrange("p (h d) -> p h d", d=64),
                                                              in0=qn[:, 0:384].rearrange("p (h d) -> p h d", d=64),
                                                              in1=hgl[:, 0, :].unsqueeze(1).to_broadcast([P, 6, 64]), op=ALU.mult),
                             reads=['qn', hk], writes=['qn'])
                        S.op('dve', lambda e: e.tensor_tensor(out=qn[:, 384:768].rearrange("p (h d) -> p h d", d=64),
                                                              in0=qn[:, 384:768].rearrange("p (h d) -> p h d", d=64),
                                                              in1=hgl[:, 1, :].unsqueeze(1).to_broadcast([P, 6, 64]), op=ALU.mult),
                             reads=['qn', hk], writes=['qn'])
                        S.op('dve', lambda e: e.tensor_tensor(out=qn[:, 1024:1088], in0=qn[:, 1024:1088], in1=hgl[:, 2, :], op=ALU.mult),
                             reads=['qn', hk], writes=['qn'])
                        S.op('pool', lambda e: e.tensor_copy(out=qn[:, 768:1024], in_=pj[:, 768:1024]), reads=['pj'], writes=['qn'])
                        S.op('pool', lambda e: e.tensor_copy(out=qkb[:, 0:1088], in_=qn[:, 0:1088]), reads=['qn'], writes=['qkb'])
                        chk('A_n%d' % t)
                        qnv = qn[:, 0:1088].rearrange("p (h d) -> p h d", d=64)
                        qsw = bass.AP(tensor=qn[:].tensor, offset=qn[:, 8:9].offset, ap=[list(qn[:].ap[0]), [64, 17], [-8, 2], [1, 8]])
                        S.op('dve', lambda e: e.tensor_tensor(out=t1[:], in0=qnv[:, :, 0:16],
                                                              in1=cs_all[:, t, :].unsqueeze(1).to_broadcast([P, 17, 16]), op=ALU.mult),
                             reads=['qn', 'cs_all'], writes=['t1'])
                        S.op('dve', lambda e: e.tensor_tensor(out=t2[:].rearrange("p h (two e) -> p h two e", two=2), in0=qsw,
                                                              in1=sn_all[:, t, :].rearrange("p (two e) -> p two e", two=2).unsqueeze(1).to_broadcast([P, 17, 2, 8]),
                                                              op=ALU.mult),
                             reads=['qn', 'sn_all'], writes=['t2'])
                        S.op('dve', lambda e: e.tensor_tensor(out=qkb[:, 0:1088].rearrange("p (h d) -> p h d", d=64)[:, :, 0:16],
                                                              in0=t1[:], in1=t2[:], op=ALU.add),
                             reads=['t1', 't2'], writes=['qkb'])
                        chk('A_r%d' % t)
                        pT, pk = next_psT()
                        for i in range(8):
                            S.op('pe', lambda e, i=i: e.transpose(out=pT[:, i * P:(i + 1) * P], in_=qkb[:, i * P:(i + 1) * P], identity=ident[:]),
                                 reads=['qkb', 'ident'], writes=[pk], inc=(i == 7))
                        S.op('dve', lambda e: e.tensor_copy(out=qT_t[b][:].rearrange("p c k -> p (c k)"), in_=pT[:, 0:384]),
                             reads=[pk], writes=[('qT_t', b)])
                        S.op('dve', lambda e: e.tensor_copy(out=kT[:, :, t * P:(t + 1) * P], in_=pT[:, 384:768].rearrange("p (c k) -> p c k", c=3)),
                             reads=[pk], writes=[('kT', t)])
                        S.op('act', lambda e: e.activation(out=qiT_t[b][:].rearrange("p c k -> p (c k)"), in_=pT[:, 768:1024], func=AF.Copy),
                             reads=[pk], writes=[('qiT_t', b)])
                        chk('A_t%d' % t)
                        pT2, pk2 = next_psT()
                        S.op('pe', lambda e: e.transpose(out=pT2[0:64, 0:P], in_=qkb[:, 1024:1088], identity=ident[:]),
                             reads=['qkb', 'ident'], writes=[pk2])
                        chk('A_k%d' % t)
                        S.op('dve', lambda e: e.tensor_copy(out=kiT2[0:64, t * P:(t + 1) * P], in_=pT2[0:64, 0:P]), reads=[pk2], writes=[('kiT', t)])
                        S.op('act', lambda e: e.activation(out=kiT2[64:128, t * P:(t + 1) * P], in_=pT2[0:64, 0:P], func=AF.Copy), reads=[pk2], writes=[('kiT', t)])
                        chk('A_kk%d' % t)
                        S.op('pool', lambda e: e.tensor_copy(out=v_aug[:, t, :, 0:64], in_=pj[:, 1092:1476].rearrange("p (h d) -> p h d", d=64)),
                             reads=['pj', 'v_all'], writes=[('v', t)])
                        S.op('dve', lambda e: e.tensor_scalar(out=ws_t[:], in0=pj[:, 1088:1092], scalar1=1.0 / 16.0, scalar2=None, op0=ALU.mult),
                             reads=['pj'], writes=['ws_t'])

                        chk('A_proj%d' % t)
                        j = t
                        N = P * (j + 1)
                        for p0 in range(0, N, 512):
                            n = min(512, N - p0)
                            for h in range(4):
                                pr, r = divmod(h, 2)
                                pb = pS[ri % 2]
                                S.op('pe', lambda e: e.matmul(pb[:, 0:n], lhsT=qiT_t[b][r * 64:(r + 1) * 64, pr, :],
                                                              rhs=kiT2[r * 64:(r + 1) * 64, p0:p0 + n], start=True, stop=True),
                                     reads=[('qiT_t', b)] + [('kiT', kt) for kt in range(p0 // P, (p0 + n) // P)], writes=[('pS', ri % 2)])
                                S.op('act', lambda e: e.activation(out=rl[ri % 2][:, 0:n], in_=pb[:, 0:n], func=AF.Relu),
                                     reads=[('pS', ri % 2)], writes=[('rl', ri % 2)])
                                if h == 0:
                                    S.op('dve', lambda e: e.tensor_scalar(out=sc[:, p0:p0 + n], in0=rl[ri % 2][:, 0:n], scalar1=ws_t[:, 0:1], scalar2=None, op0=ALU.mult),
                                         reads=[('rl', ri % 2), 'ws_t'], writes=['sc'])
                                else:
                                    S.op('dve', lambda e: e.scalar_tensor_tensor(out=sc[:, p0:p0 + n], in0=rl[ri % 2][:, 0:n], scalar=ws_t[:, h:h + 1],
                                                                                 in1=sc[:, p0:p0 + n], op0=ALU.mult, op1=ALU.add),
                                         reads=[('rl', ri % 2), 'ws_t', 'sc'], writes=['sc'])
                                ri += 1
                        S.op('dve', lambda e: e.tensor_reduce(out=bst[:, 0:1], in_=sc[:, 0:N], axis=AX.X, op=ALU.max, apply_absolute_value=True),
                             reads=['sc'], writes=['bst'])
                        S.op('pool', lambda e: e.memset(sc[0:64, N - 64:N], NEG), reads=['bst'], writes=['sc'])
                        S.op('dve', lambda e: e.tensor_scalar(out=Bk[:], in0=pw[:], scalar1=bst[:, 0:1], scalar2=None, op0=ALU.mult),
                             reads=['bst', 'pw'], writes=['Bk'])
                        S.op('dve', lambda e: e.memset(bst[:, 1:2], 0.0), reads=['bst'], writes=['bst'])
                        for k in range(NBIS + 1):
                            S.op('dve', lambda e: e.tensor_scalar(out=Mk[:, 0:N], in0=sc[:, 0:N], scalar1=bst[:, 1:2], scalar2=None,
                                                                  op0=ALU.is_ge, op1=ALU.add, accum_out=bst[:, 2:3]),
                                 reads=['sc', 'bst'], writes=['Mk', 'bst'])
                            if k < NBIS:
                                S.op('dve', lambda e: e.tensor_scalar(out=bst[:, 3:4], in0=bst[:, 2:3], scalar1=TOPK - 0.5, scalar2=-0.5,
                                                                      op0=ALU.is_ge, op1=ALU.add), reads=['bst'], writes=['bst'])
                                S.op('dve', lambda e, k=k: e.scalar_tensor_tensor(out=bst[:, 1:2], in0=bst[:, 3:4], scalar=Bk[:, k:k + 1],
                                                                                  in1=bst[:, 1:2], op0=ALU.mult, op1=ALU.add),
                                     reads=['bst', 'Bk'], writes=['bst'])
                            else:
                                S.op('dve', lambda e: e.tensor_scalar(out=bst[:, 3:4], in0=bst[:, 2:3], scalar1=TOPK - 0.5, scalar2=-1.0,
                                                                      op0=ALU.is_ge, op1=ALU.add), reads=['bst'], writes=['bst'])
                                S.op('dve', lambda e: e.scalar_tensor_tensor(out=bst[:, 4:5], in0=bst[:, 3:4], scalar=Bk[:, NBIS:NBIS + 1],
                                                                             in1=bst[:, 1:2], op0=ALU.mult, op1=ALU.add),
                                     reads=['bst', 'Bk'], writes=['bst'])
                        S.op('dve', lambda e: e.tensor_scalar(out=Mk[:, 0:N], in0=sc[:, 0:N], scalar1=bst[:, 4:5], scalar2=None, op0=ALU.is_ge),
                             reads=['sc', 'bst'], writes=['Mk'])
                        for r0 in range(0, j + 1, 8):
                            nb = min(8, j + 1 - r0)
                            pT, pk = next_psT()
                            for i in range(nb):
                                S.op('pe', lambda e, i=i: e.transpose(out=pT[:, i * P:(i + 1) * P], in_=Mk[:, (r0 + i) * P:(r0 + i + 1) * P], identity=ident[:]),
                                     reads=['Mk', 'ident'], writes=[pk], inc=(i == nb - 1))
                            S.op('act', lambda e: e.activation(out=MT[:, r0 * P:(r0 + nb) * P], in_=pT[:, 0:nb * P], func=AF.Copy),
                                 reads=[pk], writes=['MT'])
                        chk('A_idx%d' % t)
                        for h in range(6):
                            pr, r = divmod(h, 2)
                            for g0 in range(0, j + 1, 4):
                                kts = list(range(g0, min(g0 + 4, j + 1)))
                                n = len(kts) * P
                                pb = pS[ri % 2]
                                for i, kt in enumerate(kts):
                                    S.op('pe', lambda e, i=i, kt=kt: e.matmul(pb[:, i * P:(i + 1) * P], lhsT=kT[r * 64:(r + 1) * 64, pr, kt * P:(kt + 1) * P],
                                                                              rhs=qT_t[b][r * 64:(r + 1) * 64, pr, :], start=True, stop=True),
                                         reads=[('kT', kt), ('qT_t', b)], writes=[('pS', ri % 2)], inc=(i == len(kts) - 1))
                                S.op('act', lambda e: e.activation(out=Eb[gi % 2][:, 0:n], in_=pb[:, 0:n], func=AF.Exp, scale=0.125),
                                     reads=[('pS', ri % 2)], writes=[('Eb', gi % 2)])
                                S.op('dve', lambda e: e.tensor_tensor(out=PT[gi % 2][:, 0:n], in0=Eb[gi % 2][:, 0:n], in1=MT[:, g0 * P:g0 * P + n], op=ALU.mult),
                                     reads=[('Eb', gi % 2), 'MT'], writes=[('PT', gi % 2)])
                                for i, kt in enumerate(kts):
                                    S.op('pe', lambda e, i=i, kt=kt: e.matmul(pO[:, h * 65:(h + 1) * 65], lhsT=PT[gi % 2][:, i * P:(i + 1) * P],
                                                                              rhs=v_aug[:, kt, h, :], start=(kt == 0), stop=(kt == j)),
                                         reads=[('PT', gi % 2), ('v', kt)], writes=['pO'], inc=(i == len(kts) - 1))
                                gi += 1
                                ri += 1
                        normalize_cat(6, catb, rden)
                        out_proj(j, catb, catT, 3, Wo, 0)
                        chk('A_att%d' % t)
                    S.barrier()
                chk('A')

                load_w(Wo[:, 0:3, :], wov[:, 3:6, :], 'Wo')
                load_w(W1[:, :, 0:256], wv[:, :, 2628:2884], 'W1')
                load_w(W1[:, :, 256:768], w_view(w_mkv_d, l)[:, :, 0:512], 'W1')
                with ExitStack() as sg:
                    expb = sb("expb", [P, 6, 5, P], F32, sg)
                    Ef = [sb("Ef%d" % i, [P, 512], F32, sg) for i in range(2)]
                    S.dma('sp', out=expb[:].rearrange("p h m q -> p (h m q)"), in_=bias_d[l].rearrange("p h m q -> p (h m q)"), writes=['expb'])
                    S.op('act', lambda e: e.activation(out=expb[:].rearrange("p h m q -> p (h m q)"), in_=expb[:].rearrange("p h m q -> p (h m q)"), func=AF.Exp),
                         reads=['expb'], writes=['expb'])
                    S.op('pool', lambda e: e.memset(expb[64:128, :, 0, 0:64], 0.0), reads=['expb'], writes=['expb'])
                    S.op('pool', lambda e: e.memset(expb[0:64, :, 4, 64:128], 0.0), reads=['expb'], writes=['expb'])
                    gi = 0
                    ri = 0
                    for t in range(NT):
                        b = t % 2
                        make_hT(t, hTt[b][:], ('hTt', b))
                        for s_, (c0, c1) in enumerate(((0, 512), (512, 1024), (1024, 1152))):
                            pb = pP[s_ % 2]
                            for c in range(DC):
                                S.op('pe', lambda e, c=c: e.matmul(pb[:, 0:c1 - c0], lhsT=hTt[b][:, c, :], rhs=W2[:, c, c0:c1],
                                                                   start=(c == 0), stop=(c == DC - 1)),
                                     reads=[('hTt', b), 'W2'], writes=[('pP', s_ % 2)], inc=(c == DC - 1))
                            S.op('act', lambda e: e.activation(out=pj[:, c0:c1], in_=pb[:, 0:c1 - c0], func=AF.Copy),
                                 reads=[('pP', s_ % 2)], writes=['pj'])
                        S.op('act', lambda e: e.activation(out=sq[:, 0:768], in_=pj[:, 0:768], func=AF.Square), reads=['pj'], writes=['sq'])
                        head_rstd(sq[:, 0:768], 12, ssh, rsh)
                        S.op('dve', lambda e: e.tensor_tensor(out=qn[:, 0:768].rearrange("p (h d) -> p h d", d=64),
                                                              in0=pj[:, 0:768].rearrange("p (h d) -> p h d", d=64),
                                                              in1=rsh[:, 0:12].unsqueeze(2).to_broadcast([P, 12, 64]), op=ALU.mult),
                             reads=['pj', 'rsh'], writes=['qn'])
                        S.op('dve', lambda e: e.tensor_tensor(out=qkb[:, 0:384].rearrange("p (h d) -> p h d", d=64),
                                                              in0=qn[:, 0:384].rearrange("p (h d) -> p h d", d=64),
                                                              in1=hgl[:, 3, :].unsqueeze(1).to_broadcast([P, 6, 64]), op=ALU.mult),
                             reads=['qn', hk], writes=['qkb'])
                        S.op('dve', lambda e: e.tensor_tensor(out=qkb[:, 384:768].rearrange("p (h d) -> p h d", d=64),
                                                              in0=qn[:, 384:768].rearrange("p (h d) -> p h d", d=64),
                                                              in1=hgl[:, 4, :].unsqueeze(1).to_broadcast([P, 6, 64]), op=ALU.mult),
                             reads=['qn', hk], writes=['qkb'])
                        pT, pk = next_psT()
                        for i in range(6):
                            S.op('pe', lambda e, i=i: e.transpose(out=pT[:, i * P:(i + 1) * P], in_=qkb[:, i * P:(i + 1) * P], identity=ident[:]),
                                 reads=['qkb', 'ident'], writes=[pk], inc=(i == 5))
                        S.op('dve', lambda e: e.tensor_copy(out=qT_t[b][:].rearrange("p c k -> p (c k)"), in_=pT[:, 0:384]),
                             reads=[pk], writes=[('qT_t', b)])
                        S.op('act', lambda e: e.activation(out=kT[:, :, t * P:(t + 1) * P], in_=pT[:, 384:768].rearrange("p (c k) -> p c k", c=3), func=AF.Copy),
                             reads=[pk], writes=[('kT', t)])
                        S.op('pool', lambda e: e.tensor_copy(out=v_aug[:, t, :, 0:64], in_=pj[:, 768:1152].rearrange("p (h d) -> p h d", d=64)),
                             reads=['pj', 'v_all'], writes=[('v', t)])
                        j = t
                        ms = [m for m in range(5) if j - m >= 0]
                        for h in range(6):
                            pr, r = divmod(h, 2)
                            for chunk in (ms[0:4], ms[4:5]):
                                if not chunk:
                                    continue
                                n = len(chunk) * P
                                pb = pS[ri % 2]
                                for i, m in enumerate(chunk):
                                    kt = j - m
                                    S.op('pe', lambda e, i=i, kt=kt: e.matmul(pb[:, i * P:(i + 1) * P], lhsT=kT[r * 64:(r + 1) * 64, pr, kt * P:(kt + 1) * P],
                                                                              rhs=qT_t[b][r * 64:(r + 1) * 64, pr, :], start=True, stop=True),
                                         reads=[('kT', kt), ('qT_t', b)], writes=[('pS', ri % 2)], inc=(i == len(chunk) - 1))
                                S.op('act', lambda e: e.activation(out=Ef[gi % 2][:, 0:n], in_=pb[:, 0:n], func=AF.Exp, scale=0.125),
                                     reads=[('pS', ri % 2)], writes=[('Ef', gi % 2)])
                                m0 = chunk[0]
                                S.op('dve', lambda e: e.tensor_tensor(out=PT[gi % 2][:, 0:n].rearrange("p (m q) -> p m q", q=P),
                                                                      in0=Ef[gi % 2][:, 0:n].rearrange("p (m q) -> p m q", q=P),
                                                                      in1=expb[:, h, m0:m0 + len(chunk), :], op=ALU.mult),
                                     reads=[('Ef', gi % 2), 'expb'], writes=[('PT', gi % 2)])
                                for i, m in enumerate(chunk):
                                    kt = j - m
                                    S.op('pe', lambda e, i=i, kt=kt, m=m: e.matmul(pO[:, h * 65:(h + 1) * 65], lhsT=PT[gi % 2][:, i * P:(i + 1) * P],
                                                                                   rhs=v_aug[:, kt, h, :], start=(m == ms[0]), stop=(m == ms[-1])),
                                         reads=[('PT', gi % 2), ('v', kt)], writes=['pO'], inc=(i == len(chunk) - 1))
                                gi += 1
                                ri += 1
                        normalize_cat(6, catb, rden)
                        out_proj(j, catb, catT, 3, Wo, 0)
                    S.barrier()

                chk('B')
                with ExitStack() as sg:
                    mem_sb = sb("mem_sb", [P, 2, D], F32, sg)
                    memT = sb("memT", [P, DC, 256], BF16, sg)
                    mkT = sb("mkT", [P, 2, 256], BF16, sg)
                    mv_aug = sb("mv_aug", [P, 2, 4, 65], BF16, sg)
                    mkv = sb("mkv", [P, 512], F32, sg)
                    load_w(Wo[:, 0:2, :], wov[:, 6:8, :], 'Wo')
                    load_gbc(g_mem_d[l:l + 1, :])
                    S.op('pool', lambda e: e.memset(mv_aug[:], 1.0), writes=['mv_aug'])
                    for mt in range(2):
                        S.dma('sp', out=mem_sb[:, mt, :], in_=mem_d[mt * P:(mt + 1) * P, :], writes=[('mem', mt)])
                    for mt in range(2):
                        b = mt % 2
                        S.op('dve', lambda e: e.scalar_tensor_tensor(out=hb[b][:], in0=mem_sb[:, mt, :], scalar=rstd_mem[:, mt:mt + 1],
                                                                     in1=gbc[:], op0=ALU.mult, op1=ALU.mult),
                             reads=[('mem', mt), 'rstd_mem', 'gbc'], writes=[('hb', b)])
                        pT, pk = next_psT()
                        for c in range(DC):
                            S.op('pe', lambda e, c=c: e.transpose(out=pT[:, c * P:(c + 1) * P], in_=hb[b][:, c * P:(c + 1) * P], identity=ident[:]),
                                 reads=[('hb', b), 'ident'], writes=[pk], inc=(c == DC - 1))
                        S.op('act', lambda e: e.activation(out=memT[:, :, mt * P:(mt + 1) * P], in_=pT[:].rearrange("p (c k) -> p c k", c=DC), func=AF.Copy),
                             reads=[pk], writes=['memT'])
                    for mt in range(2):
                        pb = pP[mt % 2]
                        for c in range(DC):
                            S.op('pe', lambda e, c=c: e.matmul(pb[:, 0:512], lhsT=memT[:, c, mt * P:(mt + 1) * P], rhs=W1[:, c, 256:768],
                                                               start=(c == 0), stop=(c == DC - 1)),
                                 reads=['memT', 'W1'], writes=[('pP', mt % 2)], inc=(c == DC - 1))
                        S.op('act', lambda e: e.activation(out=mkv[:], in_=pb[:, 0:512], func=AF.Copy), reads=[('pP', mt % 2)], writes=['mkv'])
                        S.op('act', lambda e: e.activation(out=sq[:, 0:256], in_=mkv[:, 0:256], func=AF.Square), reads=['mkv'], writes=['sq'])
                        head_rstd(sq[:, 0:256], 4, ssh, rsh)
                        S.op('dve', lambda e: e.tensor_tensor(out=qn[:, 0:256].rearrange("p (h d) -> p h d", d=64),
                                                              in0=mkv[:, 0:256].rearrange("p (h d) -> p h d", d=64),
                                                              in1=rsh[:, 0:4].unsqueeze(2).to_broadcast([P, 4, 64]), op=ALU.mult),
                             reads=['mkv', 'rsh'], writes=['qn'])
                        S.op('dve', lambda e: e.tensor_tensor(out=qkb[:, 0:256].rearrange("p (h d) -> p h d", d=64),
                                                              in0=qn[:, 0:256].rearrange("p (h d) -> p h d", d=64),
                                                              in1=hgl[:, 6, :].unsqueeze(1).to_broadcast([P, 4, 64]), op=ALU.mult),
                             reads=['qn', hk], writes=['qkb'])
                        pT, pk = next_psT()
                        for i in range(2):
                            S.op('pe', lambda e, i=i: e.transpose(out=pT[:, i * P:(i + 1) * P], in_=qkb[:, i * P:(i + 1) * P], identity=ident[:]),
                                 reads=['qkb', 'ident'], writes=[pk], inc=(i == 1))
                        S.op('dve', lambda e: e.tensor_copy(out=mkT[:, :, mt * P:(mt + 1) * P], in_=pT[:, 0:256].rearrange("p (c k) -> p c k", c=2)),
                             reads=[pk], writes=['mkT'])
                        S.op('pool', lambda e: e.tensor_copy(out=mv_aug[:, mt, :, 0:64], in_=mkv[:, 256:512].rearrange("p (h d) -> p h d", d=64)),
                             reads=['mkv', 'mv_aug'], writes=['mv_aug'])
                    chk('M_mem')
                    load_gbc(g_mix_d[l:l + 1, :])
                    gi = 0
                    ri = 0
                    for t in range(NT):
                        if t > 0:
                            chk('M_t%d' % (t - 1))
                        b = t % 2
                        make_hT(t, hTt[b][:], ('hTt', b))
                        pb = pP[t % 2]
                        for c in range(DC):
                            S.op('pe', lambda e, c=c: e.matmul(pb[:, 0:256], lhsT=hTt[b][:, c, :], rhs=W1[:, c, 0:256],
                                                               start=(c == 0), stop=(c == DC - 1)),
                                 reads=[('hTt', b), 'W1'], writes=[('pP', t % 2)], inc=(c == DC - 1))
                        S.op('act', lambda e: e.activation(out=pj[:, 0:256], in_=pb[:, 0:256], func=AF.Copy), reads=[('pP', t % 2)], writes=['pj'])
                        S.op('act', lambda e: e.activation(out=sq[:, 0:256], in_=pj[:, 0:256], func=AF.Square), reads=['pj'], writes=['sq'])
                        head_rstd(sq[:, 0:256], 4, ssh, rsh)
                        S.op('dve', lambda e: e.tensor_tensor(out=qn[:, 0:256].rearrange("p (h d) -> p h d", d=64),
                                                              in0=pj[:, 0:256].rearrange("p (h d) -> p h d", d=64),
                                                              in1=rsh[:, 0:4].unsqueeze(2).to_broadcast([P, 4, 64]), op=ALU.mult),
                             reads=['pj', 'rsh'], writes=['qn'])
                        S.op('dve', lambda e: e.tensor_tensor(out=qkb[:, 0:256].rearrange("p (h d) -> p h d", d=64),
                                                              in0=qn[:, 0:256].rearrange("p (h d) -> p h d", d=64),
                                                              in1=hgl[:, 5, :].unsqueeze(1).to_broadcast([P, 4, 64]), op=ALU.mult),
                             reads=['qn', hk], writes=['qkb'])
                        pT, pk = next_psT()
                        for i in range(2):
                            S.op('pe', lambda e, i=i: e.transpose(out=pT[:, i * P:(i + 1) * P], in_=qkb[:, i * P:(i + 1) * P], identity=ident[:]),
                                 reads=['qkb', 'ident'], writes=[pk], inc=(i == 1))
                        S.op('dve', lambda e: e.tensor_copy(out=qT_t[b][:, 0:2, :].rearrange("p c k -> p (c k)"), in_=pT[:, 0:256]),
                             reads=[pk], writes=[('qT_t', b)])
                        chk('M_q%d' % t)
                        for h2 in range(2):
                            pb2 = pS[ri % 2]
                            blocks = [(h2 + 2 * hh, kt) for hh in range(2) for kt in range(2)]
                            for i, (h, kt) in enumerate(blocks):
                                pr, r = divmod(h, 2)
                                S.op('pe', lambda e, i=i, kt=kt, pr=pr, r=r: e.matmul(pb2[:, i * P:(i + 1) * P], lhsT=mkT[r * 64:(r + 1) * 64, pr, kt * P:(kt + 1) * P],
                                                                                      rhs=qT_t[b][r * 64:(r + 1) * 64, pr, :], start=True, stop=True),
                                     reads=['mkT', ('qT_t', b)], writes=[('pS', ri % 2)], inc=(i == 3))
                            S.op('act', lambda e: e.activation(out=Eb[gi % 2][:, :], in_=pb2[:, :], func=AF.Exp, scale=0.125),
                                 reads=[('pS', ri % 2)], writes=[('Eb', gi % 2)])
                            chk('M_s%d_%d' % (t, h2))
                            for i, (h, kt) in enumerate(blocks):
                                S.op('pe', lambda e, i=i, kt=kt, h=h: e.matmul(pO[:, h * 65:(h + 1) * 65], lhsT=Eb[gi % 2][:, i * P:(i + 1) * P],
                                                                               rhs=mv_aug[:, kt, h, :], start=(kt == 0), stop=(kt == 1)),
                                     reads=[('Eb', gi % 2), 'mv_aug'], writes=['pO'], inc=(i == 3))
                            chk('M_p%d_%d' % (t, h2))
                            gi += 1
                            ri += 1
                        chk('M_v%d' % t)
                        normalize_cat(4, catb, rden)
                        chk('M_n%d' % t)
                        out_proj(t, catb, catT, 2, Wo, 0)
                    S.barrier()

            chk('M')
            with ExitStack() as sf:
                hTg = sb("hTg", [P, DC, 512], BF16, sf)
                actT = sb("actT", [P, FB, 512], BF16, sf)
                Wd = sb("Wd", [P, FB, D], BF16, sf)
                Wg = [sb("Wg%d" % i, [P, DC, 512], BF16, sf) for i in range(2)]
                Wu = [sb("Wu%d" % i, [P, DC, 512], BF16, sf) for i in range(2)]
                sg_ = [sb("sg%d" % i, [P, 512], F32, sf) for i in range(2)]
                load_gbc(g_ffn_d[l:l + 1, :])
                norm_stats()
                wgu = w_view(w_gu_d, l)
                wdv = w_view(w_dn_d, l)
                si = 0
                fi = 0
                chk('F_n')
                for G in range(4):
                    for tt in range(4):
                        t = 4 * G + tt
                        make_hT(t, hTg[:, :, tt * P:(tt + 1) * P], 'hTg')
                    chk('F_h%d' % G)
                    for s_ in range(6):
                        f0 = s_ * 512
                        nf = min(512, DFF - f0)
                        wb = si % 2
                        load_w(Wg[wb][:, :, 0:nf], wgu[:, :, f0:f0 + nf], ('Wg', wb))
                        load_w(Wu[wb][:, :, 0:nf], wgu[:, :, DFF + f0:DFF + f0 + nf], ('Wu', wb))
                        if G == 0 and s_ == 1:
                            for q4 in range(4):
                                load_w(Wd[:, :, q4 * 256:(q4 + 1) * 256], wdv[:, :, q4 * 256:(q4 + 1) * 256], 'Wd')
                        for i in range(nf // P):
                            fb = (f0 // P) + i
                            pg = pP[fi % 2]
                            pu = pS[fi % 2]
                            for c in range(DC):
                                S.op('pe', lambda e, c=c: e.matmul(pg[:, :], lhsT=Wg[wb][:, c, i * P:(i + 1) * P], rhs=hTg[:, c, :],
                                                                   start=(c == 0), stop=(c == DC - 1)),
                                     reads=[('Wg', wb), 'hTg'], writes=[('pP', fi % 2)], inc=(c == DC - 1))
                            for c in range(DC):
                                S.op('pe', lambda e, c=c: e.matmul(pu[:, :], lhsT=Wu[wb][:, c, i * P:(i + 1) * P], rhs=hTg[:, c, :],
                                                                   start=(c == 0), stop=(c == DC - 1)),
                                     reads=[('Wu', wb), 'hTg'], writes=[('pS', fi % 2)], inc=(c == DC - 1))
                            S.op('act', lambda e: e.activation(out=sg_[fi % 2][:], in_=pg[:, :], func=AF.Silu),
                                 reads=[('pP', fi % 2)], writes=[('sg', fi % 2)])
                            S.op('dve', lambda e: e.tensor_tensor(out=actT[:, fb, :], in0=sg_[fi % 2][:], in1=pu[:, :], op=ALU.mult),
                                 reads=[('sg', fi % 2), ('pS', fi % 2)], writes=['actT'])
                            fi += 1
                        si += 1
                        chk('F_g%d_%d' % (G, s_))
                    for tt in range(4):
                        t = 4 * G + tt
                        for half in range(2):
                            pb = pO if half == 0 else pY
                            pkey = 'pO' if half == 0 else 'pY'
                            for fb in range(FB):
                                S.op('pe', lambda e, fb=fb: e.matmul(pb[:, :], lhsT=actT[:, fb, tt * P:(tt + 1) * P],
                                                                     rhs=Wd[:, fb, half * 512:(half + 1) * 512],
                                                                     start=(fb == 0), stop=(fb == FB - 1)),
                                     reads=['actT', 'Wd'], writes=[pkey], inc=(fb == FB - 1))
                            S.op('dve', lambda e: e.tensor_tensor(out=x_sb[:, t, half * 512:(half + 1) * 512],
                                                                  in0=x_sb[:, t, half * 512:(half + 1) * 512], in1=pb[:, :], op=ALU.add),
                                 reads=[pkey, ('x', t)], writes=[('x', t)])
                S.barrier()

        finish()
    return nc


def _host_layout(inputs):
    f = lambda a: np.ascontiguousarray(np.asarray(a))
    x = f(inputs["x"]); mem = f(inputs["mem"]); pos = f(inputs["positions"]).astype(np.int32)
    hgains = np.stack([f(inputs[k]) for k in ("g_q_a", "g_k_a", "g_k_idx", "g_q_b", "g_k_b", "g_q_m", "g_k_m")], axis=1)
    rb = f(inputs["rel_bias"])
    k = np.arange(P)[:, None, None]
    m = np.arange(5)[None, :, None]
    q = np.arange(P)[None, None, :]
    idx = np.clip(128 * m + q - k, -256, 256) + 256
    biasT = rb[:, :, idx]
    biasT = np.ascontiguousarray(biasT.transpose(0, 2, 1, 3, 4)).astype(np.float32)
    shared = {
        "g_mix": f(inputs["g_mix"]), "g_ffn": f(inputs["g_ffn"]), "g_mem": f(inputs["g_mem"]),
        "hgains": np.ascontiguousarray(hgains.astype(np.float32)),
        "w_in": f(inputs["w_in"]), "w_mem_kv": f(inputs["w_mem_kv"]), "w_out": f(inputs["w_out"]),
        "w_gate_up": f(inputs["w_gate_up"]), "w_down": f(inputs["w_down"]), "biasT": biasT,
    }
    in_maps = []
    for b in range(8):
        d = dict(shared)
        d["x"] = x[b]
        d["mem"] = mem[b]
        d["pos"] = np.ascontiguousarray(pos[b].reshape(NT, P).T)
        in_maps.append(d)
    return in_maps


def build_safe(depth=DEPTH, stop=None):
    try:
        return build(depth, stop)
    except _Stop:
        return _Stop.nc


def kernel(**inputs):
    depth = int(os.environ.get("KDEPTH", DEPTH))
    stop = os.environ.get("KSTOP") or None
    ncores = int(os.environ.get("KCORES", 8))
    nc = build_safe(depth, stop)
    in_maps = _host_layout(inputs)[:ncores]
    res = run_bass_kernel_spmd(nc, in_maps, core_ids=list(range(ncores)))
    out = np.stack([np.asarray(r["out"]) for r in res.results], axis=0)
    return out.astype(np.float32)
```

```python
import math
import os
from contextlib import ExitStack

import numpy as np
import concourse.bass as bass
import concourse.mybir as mybir
from concourse.bass_utils import run_bass_kernel_spmd

F32 = mybir.dt.float32
BF16 = mybir.dt.bfloat16
I32 = mybir.dt.int32
ALU = mybir.AluOpType
AF = mybir.ActivationFunctionType
AX = mybir.AxisListType

P = 128
SEQ = 2048
NT = 16
D = 1024
DC = 8
DFF = 2816
FB = 22
DEPTH = 4
D_IN = 2884
EPS = 1e-6
NBIS = 16
TOPK = 256
NEG = -1.0e30
ENGS = ('pe', 'act', 'dve', 'pool', 'sp')


class Sync:
    def __init__(self, nc, es, nslots=8):
        self.nc = nc
        self.eng = {'pe': nc.tensor, 'act': nc.scalar, 'dve': nc.vector, 'pool': nc.gpsimd, 'sp': nc.sync}
        self.sem = {}
        self.cnt = {}
        for k in ENGS:
            self.sem[k] = es.enter_context(nc.semaphore("s_" + k))
            self.cnt[k] = 0
        self.slots = {}
        for q in ('sp', 'pool'):
            self.slots[q] = []
            for i in range(nslots):
                nm = "d%s%d" % (q, i)
                self.sem[nm] = es.enter_context(nc.semaphore(nm))
                self.cnt[nm] = 0
                self.slots[q].append(nm)
        self.slot_next = {'sp': 0, 'pool': 0}
        self.seen = {k: {} for k in ENGS}
        self.writer = {}
        self.readers = {}
        self.n_ins = 0
        self.n_wait = 0

    def _need(self, e, deps):
        best = {}
        for (x, v) in deps:
            if v > best.get(x, 0):
                best[x] = v
        for x, v in best.items():
            if self.seen[e].get(x, 0) >= v:
                continue
            self.eng[e].wait_ge(self.sem[x], v)
            self.seen[e][x] = v
            self.n_wait += 1

    def _deps(self, e, reads, writes):
        deps = []
        for k in reads:
            w = self.writer.get(k)
            if w is not None:
                deps.append(w)
        for k in writes:
            w = self.writer.get(k)
            if w is not None and w[0] != e:
                deps.append(w)
            for (x, v) in self.readers.get(k, {}).items():
                if x != e:
                    deps.append((x, v))
        return deps

    @staticmethod
    def _is_psum(k):
        return (isinstance(k, tuple) and k[0] in ('pP', 'pS', 'psT')) or k in ('pO', 'pY')

    def op(self, e, fn, reads=(), writes=(), inc=True):
        writes = list(writes) + [k for k in reads if self._is_psum(k) and k not in writes]
        self._need(e, self._deps(e, reads, writes))
        ins = fn(self.eng[e])
        self.n_ins += 1
        if inc:
            self.cnt[e] += 1
            ins.then_inc(self.sem[e], 1)
            c = self.cnt[e]
        else:
            c = self.cnt[e] + 1
        for k in reads:
            self.readers.setdefault(k, {})[e] = c
        for k in writes:
            self.writer[k] = (e, c)
            self.readers[k] = {}
        return ins

    def dma(self, q, out, in_, reads=(), writes=(), **kw):
        slot = self.slots[q][self.slot_next[q] % len(self.slots[q])]
        self.slot_next[q] += 1
        deps = self._deps(None, reads, writes)
        deps.append((slot, self.cnt[slot]))
        self._need(q, deps)
        ins = self.eng[q].dma_start(out=out, in_=in_, **kw)
        self.n_ins += 1
        self.cnt[slot] += 16
        ins.then_inc(self.sem[slot], 16)
        c = self.cnt[slot]
        for k in reads:
            self.readers.setdefault(k, {})[slot] = c
        for k in writes:
            self.writer[k] = (slot, c)
            self.readers[k] = {}
        return ins

    def barrier(self):
        for e in ENGS:
            self._need(e, [(x, self.cnt[x]) for x in ENGS if x != e])

    def drain_dma(self, e):
        self._need(e, [(s, self.cnt[s]) for q in self.slots for s in self.slots[q]])


class _Stop(Exception):
    pass


def build(depth=DEPTH, stop=None):
    nc = bass.Bass("TRN2", target_bir_lowering=False)

    def din(name, shape, dt=F32):
        return nc.dram_tensor(name, list(shape), dt, kind="ExternalInput").ap()

    x_d = din("x", [SEQ, D])
    mem_d = din("mem", [256, D])
    pos_d = din("pos", [P, NT], I32)
    g_mix_d = din("g_mix", [DEPTH, D])
    g_ffn_d = din("g_ffn", [DEPTH, D])
    g_mem_d = din("g_mem", [DEPTH, D])
    hg_d = din("hgains", [DEPTH, 7, 64])
    w_in_d = din("w_in", [DEPTH, D, D_IN])
    w_mkv_d = din("w_mem_kv", [DEPTH, D, 512])
    w_out_d = din("w_out", [DEPTH, D, D])
    w_gu_d = din("w_gate_up", [DEPTH, D, 2 * DFF])
    w_dn_d = din("w_down", [DEPTH, DFF, D])
    bias_d = din("biasT", [DEPTH, P, 6, 5, P])
    out_d = nc.dram_tensor("out", [SEQ, D], F32, kind="ExternalOutput").ap()

    with ExitStack() as es:
        S = Sync(nc, es)

        def finish():
            for t in range(NT):
                S.dma('sp', out=out_d[t * P:(t + 1) * P, :], in_=x_sb[:, t, :], reads=[('x', t)])
            S.drain_dma('sp')
            S.barrier()
            print("kernel build: instructions=%d waits=%d sbuf_left=%d" % (S.n_ins, S.n_wait, nc.sbuf_bytes_remaining))

        def chk(name):
            if stop == name:
                S.barrier()
                finish()
                _Stop.nc = nc
                raise _Stop()

        uniq = [0]

        def sb(name, shape, dt, stack=es):
            uniq[0] += 1
            return stack.enter_context(nc.sbuf_tensor("%s_%d" % (name, uniq[0]), list(shape), dt))

        def ps(name, shape, dt):
            return es.enter_context(nc.psum_tensor(name, list(shape), dt))

        x_sb = sb("x_sb", [P, NT, D], F32)
        ident = sb("ident", [P, P], BF16)
        identf = sb("identf", [P, P], F32)
        cs_all = sb("cs_all", [P, NT, 16], F32)
        sn_all = sb("sn_all", [P, NT, 16], F32)
        cneg = sb("cneg", [P, 32], F32)
        pw = sb("pw", [P, NBIS + 1], F32)
        ss = sb("ss", [P, NT], F32)
        rs1 = sb("rs1", [P, NT], F32)
        rstd = sb("rstd", [P, NT], F32)
        rstd_mem = sb("rstd_mem", [P, 2], F32)
        small = sb("small", [P, 64], F32)
        gbc = sb("gbc", [P, D], F32)
        hg = [sb("hg%d" % i, [P, 7, 64], F32) for i in range(2)]
        hb = [sb("hb%d" % i, [P, D], BF16) for i in range(2)]
        sqj = sb("sqj", [P, D], BF16)

        pP = [ps("pP%d" % i, [P, 512], F32) for i in range(2)]
        pS = [ps("pS%d" % i, [P, 512], F32) for i in range(2)]
        pO = ps("pO", [P, 512], F32)
        pY = ps("pY", [P, 512], F32)
        psT = [ps("psT%d" % i, [P, 1024], BF16) for i in range(2)]
        psT_ctr = [0]

        def next_psT():
            i = psT_ctr[0] % 2
            psT_ctr[0] += 1
            return psT[i], ('psT', i)

        for t in range(NT):
            S.dma('sp', out=x_sb[:, t, :], in_=x_d[t * P:(t + 1) * P, :], writes=[('x', t)])

        S.op('pool', lambda e: e.memset(identf[:], 0.0), writes=['identf'])
        S.op('pool', lambda e: e.affine_select(out=identf[:], in_=identf[:], pattern=[[-1, P]],
                                               compare_op=ALU.not_equal, fill=1.0, base=0,
                                               channel_multiplier=1), reads=['identf'], writes=['identf'])
        S.op('pool', lambda e: e.tensor_copy(out=ident[:], in_=identf[:]), reads=['identf'], writes=['ident'])
        S.op('pool', lambda e: e.memset(cneg[:], -0.5), writes=['cneg'])
        for k in range(NBIS + 1):
            S.op('pool', lambda e, k=k: e.memset(pw[:, k:k + 1], 2.0 ** (-k)), writes=['pw'])

        with ExitStack() as st:
            pos_i = sb("pos_i", [P, NT], I32, st)
            posf = sb("posf", [P, NT], F32, st)
            invf = sb("invf", [P, 8], F32, st)
            ang = sb("ang", [P, NT, 8], F32, st)
            a2 = sb("a2", [P, NT, 8], F32, st)
            kf = sb("kf", [P, NT, 8], F32, st)
            ki_ = sb("ki_", [P, NT, 8], I32, st)
            m1 = sb("m1", [P, NT, 8], F32, st)
            sv = sb("sv", [P, NT, 8], F32, st)
            S.dma('sp', out=pos_i[:], in_=pos_d[:, :], writes=['pos_i'])
            S.op('dve', lambda e: e.tensor_copy(out=posf[:], in_=pos_i[:]), reads=['pos_i'], writes=['posf'])
            for i in range(8):
                S.op('dve', lambda e, i=i: e.memset(invf[:, i:i + 1], float(500000.0 ** (-i / 8.0))), writes=['invf'])
            S.op('dve', lambda e: e.tensor_tensor(out=ang[:], in0=posf[:].unsqueeze(2).to_broadcast([P, NT, 8]),
                                                  in1=invf[:].unsqueeze(1).to_broadcast([P, NT, 8]), op=ALU.mult),
                 reads=['posf', 'invf'], writes=['ang'])
            TWO_PI = 2.0 * math.pi
            C1 = 6.28125
            C2 = TWO_PI - C1
            for which, off in (('sin', 0.0), ('cos', math.pi / 2.0)):
                S.op('dve', lambda e: e.tensor_scalar(out=a2[:], in0=ang[:], scalar1=off, scalar2=None, op0=ALU.add),
                     reads=['ang'], writes=['a2'])
                S.op('dve', lambda e: e.tensor_scalar(out=kf[:], in0=a2[:], scalar1=1.0 / TWO_PI, scalar2=None, op0=ALU.mult),
                     reads=['a2'], writes=['kf'])
                S.op('dve', lambda e: e.tensor_copy(out=ki_[:], in_=kf[:]), reads=['kf'], writes=['ki_'])
                S.op('dve', lambda e: e.tensor_copy(out=kf[:], in_=ki_[:]), reads=['ki_'], writes=['kf'])
                S.op('dve', lambda e: e.scalar_tensor_tensor(out=a2[:], in0=kf[:], scalar=-C1, in1=a2[:], op0=ALU.mult, op1=ALU.add),
                     reads=['kf', 'a2'], writes=['a2'])
                S.op('dve', lambda e: e.scalar_tensor_tensor(out=a2[:], in0=kf[:], scalar=-C2, in1=a2[:], op0=ALU.mult, op1=ALU.add),
                     reads=['kf', 'a2'], writes=['a2'])
                S.op('dve', lambda e: e.tensor_scalar(out=m1[:], in0=a2[:], scalar1=math.pi, scalar2=-TWO_PI, op0=ALU.is_gt, op1=ALU.mult),
                     reads=['a2'], writes=['m1'])
                S.op('dve', lambda e: e.tensor_tensor(out=a2[:], in0=a2[:], in1=m1[:], op=ALU.add), reads=['a2', 'm1'], writes=['a2'])
                S.op('dve', lambda e: e.tensor_scalar(out=m1[:], in0=a2[:], scalar1=-math.pi, scalar2=TWO_PI, op0=ALU.is_lt, op1=ALU.mult),
                     reads=['a2'], writes=['m1'])
                S.op('dve', lambda e: e.tensor_tensor(out=a2[:], in0=a2[:], in1=m1[:], op=ALU.add), reads=['a2', 'm1'], writes=['a2'])
                S.op('dve', lambda e: e.tensor_scalar(out=a2[:], in0=a2[:], scalar1=3.1415925, scalar2=-3.1415925, op0=ALU.min, op1=ALU.max),
                     reads=['a2'], writes=['a2'])
                S.op('act', lambda e: e.activation(out=sv[:], in_=a2[:], func=AF.Sin), reads=['a2'], writes=['sv'])
                if which == 'sin':
                    S.op('dve', lambda e: e.tensor_scalar(out=sn_all[:, :, 0:8], in0=sv[:], scalar1=-1.0, scalar2=None, op0=ALU.mult),
                         reads=['sv'], writes=['sn_all'])
                    S.op('dve', lambda e: e.tensor_copy(out=sn_all[:, :, 8:16], in_=sv[:]), reads=['sv'], writes=['sn_all'])
                else:
                    S.op('dve', lambda e: e.tensor_copy(out=cs_all[:, :, 0:8], in_=sv[:]), reads=['sv'], writes=['cs_all'])
                    S.op('dve', lambda e: e.tensor_copy(out=cs_all[:, :, 8:16], in_=sv[:]), reads=['sv'], writes=['cs_all'])
            S.barrier()

        with ExitStack() as st:
            mem_sb = sb("mem_sb0", [P, 2, D], F32, st)
            for mt in range(2):
                S.dma('sp', out=mem_sb[:, mt, :], in_=mem_d[mt * P:(mt + 1) * P, :], writes=['mem_sb'])
            for mt in range(2):
                S.op('act', lambda e, mt=mt: e.activation(out=sqj[:], in_=mem_sb[:, mt, :], func=AF.Square,
                                                          accum_out=small[:, mt:mt + 1]),
                     reads=['mem_sb'], writes=['sqj', 'small'])
            S.op('dve', lambda e: e.tensor_scalar(out=small[:, 2:4], in0=small[:, 0:2], scalar1=1.0 / D, scalar2=EPS,
                                                  op0=ALU.mult, op1=ALU.add), reads=['small'], writes=['small'])
            S.op('pool', lambda e: e.tensor_tensor(out=rstd_mem[:], in0=small[:, 2:4], in1=cneg[:, 0:2], op=ALU.pow),
                 reads=['small', 'cneg'], writes=['rstd_mem'])
            S.barrier()

        chk('setup')
        def bcast_rows(src):
            n = 1
            for s_ in src.shape:
                n *= s_
            return bass.AP(tensor=src.tensor, offset=src.offset, ap=[[0, P], [1, n]])

        def load_gbc(src2d):
            S.dma('sp', out=gbc[:], in_=bcast_rows(src2d), writes=['gbc'])

        def norm_stats():
            for t in range(NT):
                S.op('act', lambda e, t=t: e.activation(out=sqj[:], in_=x_sb[:, t, :], func=AF.Square,
                                                        accum_out=ss[:, t:t + 1]),
                     reads=[('x', t)], writes=['sqj', 'ss'])
            S.op('dve', lambda e: e.tensor_scalar(out=rs1[:], in0=ss[:], scalar1=1.0 / D, scalar2=EPS, op0=ALU.mult, op1=ALU.add),
                 reads=['ss'], writes=['rs1'])
            S.op('pool', lambda e: e.tensor_tensor(out=rstd[:], in0=rs1[:], in1=cneg[:, 0:NT], op=ALU.pow),
                 reads=['rs1', 'cneg'], writes=['rstd'])

        def make_hT(t, dst, dst_key):
            b = t % 2
            S.op('dve', lambda e: e.scalar_tensor_tensor(out=hb[b][:], in0=x_sb[:, t, :], scalar=rstd[:, t:t + 1],
                                                         in1=gbc[:], op0=ALU.mult, op1=ALU.mult),
                 reads=[('x', t), 'rstd', 'gbc'], writes=[('hb', b)])
            pT, pk = next_psT()
            for c in range(DC):
                S.op('pe', lambda e, c=c: e.transpose(out=pT[:, c * P:(c + 1) * P], in_=hb[b][:, c * P:(c + 1) * P], identity=ident[:]),
                     reads=[('hb', b), 'ident'], writes=[pk], inc=(c == DC - 1))
            S.op('act', lambda e: e.activation(out=dst, in_=pT[:].rearrange("p (c k) -> p c k", c=DC), func=AF.Copy),
                 reads=[pk], writes=[dst_key])

        def w_view(wd, l):
            return wd[l].rearrange("(c p) n -> p c n", p=P)

        def load_w(dst, src, key):
            S.dma('pool', out=dst, in_=src, writes=[key], max_dma_last_dim=4096)

        def head_rstd(sq_ap, nh, ssh, rsh):
            S.op('dve', lambda e: e.tensor_reduce(out=ssh[:, 0:nh], in_=sq_ap.rearrange("p (h d) -> p h d", d=64), axis=AX.X, op=ALU.add),
                 reads=['sq'], writes=['ssh'])
            S.op('dve', lambda e: e.tensor_scalar(out=ssh[:, 16:16 + nh], in0=ssh[:, 0:nh], scalar1=1.0 / 64, scalar2=EPS,
                                                  op0=ALU.mult, op1=ALU.add), reads=['ssh'], writes=['ssh'])
            S.op('pool', lambda e: e.tensor_tensor(out=rsh[:, 0:nh], in0=ssh[:, 16:16 + nh], in1=cneg[:, 0:nh], op=ALU.pow),
                 reads=['ssh', 'cneg'], writes=['rsh'])

        def out_proj(j, catb, catT, ncat, Wo, c_base):
            pT, pk = next_psT()
            for c in range(ncat):
                S.op('pe', lambda e, c=c: e.transpose(out=pT[:, c * P:(c + 1) * P], in_=catb[:, c * P:(c + 1) * P], identity=ident[:]),
                     reads=['catb', 'ident'], writes=[pk], inc=(c == ncat - 1))
            S.op('act', lambda e: e.activation(out=catT[:, 0:ncat * P], in_=pT[:, 0:ncat * P], func=AF.Copy),
                 reads=[pk], writes=['catT'])
            for half in range(2):
                pb = pP[half]
                for c in range(ncat):
                    S.op('pe', lambda e, c=c: e.matmul(pb[:, :], lhsT=catT[:, c * P:(c + 1) * P],
                                                       rhs=Wo[:, c_base + c, half * 512:(half + 1) * 512],
                                                       start=(c == 0), stop=(c == ncat - 1)),
                         reads=['catT', 'Wo'], writes=[('pP', half)], inc=(c == ncat - 1))
                S.op('dve', lambda e: e.tensor_tensor(out=x_sb[:, j, half * 512:(half + 1) * 512],
                                                      in0=x_sb[:, j, half * 512:(half + 1) * 512], in1=pb[:, :], op=ALU.add),
                     reads=[('pP', half), ('x', j)], writes=[('x', j)])

        def normalize_cat(nh, catb, rden):
            pOv = pO[:, 0:nh * 65].rearrange("p (h d) -> p h d", d=65)
            S.op('dve', lambda e: e.reciprocal(out=rden[:, 0:nh], in_=pOv[:, :, 64]), reads=['pO'], writes=['rden'])
            S.op('dve', lambda e: e.tensor_tensor(out=catb[:, 0:nh * 64].rearrange("p (h d) -> p h d", d=64),
                                                  in0=pOv[:, :, 0:64],
                                                  in1=rden[:, 0:nh].unsqueeze(2).to_broadcast([P, nh, 64]), op=ALU.mult),
                 reads=['pO', 'rden'], writes=['catb'])

        for l in range(depth):
            hgl = hg[l % 2]
            hk = ('hg', l % 2)
            with ExitStack() as sa:
                W1 = sb("W1", [P, DC, 1476], BF16, sa)
                W2 = sb("W2", [P, DC, 1152], BF16, sa)
                Wo = sb("Wo", [P, 3, D], BF16, sa)
                kT = sb("kT", [P, 3, SEQ], BF16, sa)
                v_aug = sb("v_aug", [P, NT, 6, 65], BF16, sa)
                hTt = [sb("hTt%d" % i, [P, DC, P], BF16, sa) for i in range(2)]
                pj = sb("pj", [P, 1476], F32, sa)
                sq = sb("sq", [P, 832], F32, sa)
                qn = sb("qn", [P, 1088], F32, sa)
                qkb = sb("qkb", [P, 1088], BF16, sa)
                qT_t = [sb("qT_t%d" % i, [P, 3, P], BF16, sa) for i in range(2)]
                ssh = sb("ssh", [P, 32], F32, sa)
                rsh = sb("rsh", [P, 16], F32, sa)
                rden = sb("rden", [P, 8], F32, sa)
                catb = sb("catb", [P, 384], BF16, sa)
                catT = sb("catT", [P, 384], BF16, sa)
                Eb = [sb("Eb%d" % i, [P, 512], BF16, sa) for i in range(2)]
                PT = [sb("PT%d" % i, [P, 512], BF16, sa) for i in range(2)]

                S.dma('sp', out=hgl[:].rearrange("p a d -> p (a d)"),
                      in_=bcast_rows(hg_d[l:l + 1]), writes=[hk])
                wv = w_view(w_in_d, l)
                load_w(W1[:, :, 0:768], wv[:, :, 0:768], 'W1')
                load_w(W1[:, :, 768:1092], wv[:, :, 1152:1476], 'W1')
                load_w(W1[:, :, 1092:1476], wv[:, :, 768:1152], 'W1')
                wov = w_view(w_out_d, l)
                load_w(Wo[:, 0:3, :], wov[:, 0:3, :], 'Wo')
                load_w(W2[:, :, 0:576], wv[:, :, 1476:2052], 'W2')
                load_w(W2[:, :, 576:1152], wv[:, :, 2052:2628], 'W2')
                load_gbc(g_mix_d[l:l + 1, :])
                S.op('pool', lambda e: e.memset(v_aug[:], 1.0), writes=['v_all'])
                norm_stats()
                chk('norm')

                with ExitStack() as sg:
                    kiT2 = sb("kiT2", [P, SEQ], BF16, sg)
                    qiT_t = [sb("qiT_t%d" % i, [P, 2, P], BF16, sg) for i in range(2)]
                    sc = sb("sc", [P, SEQ], F32, sg)
                    Mk = sb("Mk", [P, SEQ], BF16, sg)
                    MT = sb("MT", [P, SEQ], BF16, sg)
                    rl = [sb("rl%d" % i, [P, 512], F32, sg) for i in range(2)]
                    t1 = sb("t1", [P, 17, 16], F32, sg)
                    t2 = sb("t2", [P, 17, 16], F32, sg)
                    ws_t = sb("ws_t", [P, 4], F32, sg)
                    bst = sb("bst", [P, 8], F32, sg)
                    Bk = sb("Bk", [P, NBIS + 1], F32, sg)
                    gi = 0
                    ri = 0
                    for t in range(NT):
                        b = t % 2
                        make_hT(t, hTt[b][:], ('hTt', b))
                        chk('A_h%d' % t)
                        for s_, (c0, c1) in enumerate(((0, 512), (512, 1024), (1024, 1476))):
                            pb = pP[s_ % 2]
                            for c in range(DC):
                                S.op('pe', lambda e, c=c: e.matmul(pb[:, 0:c1 - c0], lhsT=hTt[b][:, c, :], rhs=W1[:, c, c0:c1],
                                                                   start=(c == 0), stop=(c == DC - 1)),
                                     reads=[('hTt', b), 'W1'], writes=[('pP', s_ % 2)], inc=(c == DC - 1))
                            S.op('act', lambda e: e.activation(out=pj[:, c0:c1], in_=pb[:, 0:c1 - c0], func=AF.Copy),
                                 reads=[('pP', s_ % 2)], writes=['pj'])
                        chk('A_p%d' % t)
                        S.op('act', lambda e: e.activation(out=sq[:, 0:768], in_=pj[:, 0:768], func=AF.Square), reads=['pj'], writes=['sq'])
                        S.op('act', lambda e: e.activation(out=sq[:, 768:832], in_=pj[:, 1024:1088], func=AF.Square), reads=['pj'], writes=['sq'])
                        head_rstd(sq[:, 0:832], 13, ssh, rsh)
                        S.op('dve', lambda e: e.tensor_tensor(out=qn[:, 0:768].rearrange("p (h d) -> p h d", d=64),
                                                              in0=pj[:, 0:768].rearrange("p (h d) -> p h d", d=64),
                                                              in1=rsh[:, 0:12].unsqueeze(2).to_broadcast([P, 12, 64]), op=ALU.mult),
                             reads=['pj', 'rsh'], writes=['qn'])
                        S.op('dve', lambda e: e.tensor_scalar(out=qn[:, 1024:1088], in0=pj[:, 1024:1088], scalar1=rsh[:, 12:13], scalar2=None, op0=ALU.mult),
                             reads=['pj', 'rsh'], writes=['qn'])
                        S.op('dve', lambda e: e.tensor_tensor(out=qn[:, 0:384].rearrange("p (h d) -> p h d", d=64),
                                                              in0=qn[:, 0:384].rearrange("p (h d) -> p h d", d=64),
                                                              in1=hgl[:, 0, :].unsqueeze(1).to_broadcast([P, 6, 64]), op=ALU.mult),
                             reads=['qn', hk], writes=['qn'])
                        S.op('dve', lambda e: e.tensor_tensor(out=qn[:, 384:768].rearrange("p (h d) -> p h d", d=64),
                                                              in0=qn[:, 384:768].rearrange("p (h d) -> p h d", d=64),
                                                              in1=hgl[:, 1, :].unsqueeze(1).to_broadcast([P, 6, 64]), op=ALU.mult),
                             reads=['qn', hk], writes=['qn'])
                        S.op('dve', lambda e: e.tensor_tensor(out=qn[:, 1024:1088], in0=qn[:, 1024:1088], in1=hgl[:, 2, :], op=ALU.mult),
                             reads=['qn', hk], writes=['qn'])
                        S.op('pool', lambda e: e.tensor_copy(out=qn[:, 768:1024], in_=pj[:, 768:1024]), reads=['pj'], writes=['qn'])
                        S.op('pool', lambda e: e.tensor_copy(out=qkb[:, 0:1088], in_=qn[:, 0:1088]), reads=['qn'], writes=['qkb'])
                        chk('A_n%d' % t)
                        qnv = qn[:, 0:1088].rearrange("p (h d) -> p h d", d=64)
                        qsw = bass.AP(tensor=qn[:].tensor, offset=qn[:, 8:9].offset, ap=[list(qn[:].ap[0]), [64, 17], [-8, 2], [1, 8]])
                        S.op('dve', lambda e: e.tensor_tensor(out=t1[:], in0=qnv[:, :, 0:16],
                                                              in1=cs_all[:, t, :].unsqueeze(1).to_broadcast([P, 17, 16]), op=ALU.mult),
                             reads=['qn', 'cs_all'], writes=['t1'])
                        S.op('dve', lambda e: e.tensor_tensor(out=t2[:].rearrange("p h (two e) -> p h two e", two=2), in0=qsw,
                                                              in1=sn_all[:, t, :].rearrange("p (two e) -> p two e", two=2).unsqueeze(1).to_broadcast([P, 17, 2, 8]),
                                                              op=ALU.mult),
                             reads=['qn', 'sn_all'], writes=['t2'])
                        S.op('dve', lambda e: e.tensor_tensor(out=qkb[:, 0:1088].rearrange("p (h d) -> p h d", d=64)[:, :, 0:16],
                                                              in0=t1[:], in1=t2[:], op=ALU.add),
                             reads=['t1', 't2'], writes=['qkb'])
                        chk('A_r%d' % t)
                        pT, pk = next_psT()
                        for i in range(8):
                            S.op('pe', lambda e, i=i: e.transpose(out=pT[:, i * P:(i + 1) * P], in_=qkb[:, i * P:(i + 1) * P], identity=ident[:]),
                                 reads=['qkb', 'ident'], writes=[pk], inc=(i == 7))
                        S.op('dve', lambda e: e.tensor_copy(out=qT_t[b][:].rearrange("p c k -> p (c k)"), in_=pT[:, 0:384]),
                             reads=[pk], writes=[('qT_t', b)])
                        S.op('dve', lambda e: e.tensor_copy(out=kT[:, :, t * P:(t + 1) * P], in_=pT[:, 384:768].rearrange("p (c k) -> p c k", c=3)),
                             reads=[pk], writes=[('kT', t)])
                        S.op('act', lambda e: e.activation(out=qiT_t[b][:].rearrange("p c k -> p (c k)"), in_=pT[:, 768:1024], func=AF.Copy),
                             reads=[pk], writes=[('qiT_t', b)])
                        chk('A_t%d' % t)
                        pT2, pk2 = next_psT()
                        S.op('pe', lambda e: e.transpose(out=pT2[0:64, 0:P], in_=qkb[:, 1024:1088], identity=ident[:]),
                             reads=['qkb', 'ident'], writes=[pk2])
                        chk('A_k%d' % t)
                        S.op('dve', lambda e: e.tensor_copy(out=kiT2[0:64, t * P:(t + 1) * P], in_=pT2[0:64, 0:P]), reads=[pk2], writes=[('kiT', t)])
                        S.op('act', lambda e: e.activation(out=kiT2[64:128, t * P:(t + 1) * P], in_=pT2[0:64, 0:P], func=AF.Copy), reads=[pk2], writes=[('kiT', t)])
                        chk('A_kk%d' % t)
                        S.op('pool', lambda e: e.tensor_copy(out=v_aug[:, t, :, 0:64], in_=pj[:, 1092:1476].rearrange("p (h d) -> p h d", d=64)),
                             reads=['pj', 'v_all'], writes=[('v', t)])
                        S.op('dve', lambda e: e.tensor_scalar(out=ws_t[:], in0=pj[:, 1088:1092], scalar1=1.0 / 16.0, scalar2=None, op0=ALU.mult),
                             reads=['pj'], writes=['ws_t'])

                        chk('A_proj%d' % t)
                        j = t
                        N = P * (j + 1)
                        for p0 in range(0, N, 512):
                            n = min(512, N - p0)
                            for h in range(4):
                                pr, r = divmod(h, 2)
                                pb = pS[ri % 2]
                                S.op('pe', lambda e: e.matmul(pb[:, 0:n], lhsT=qiT_t[b][r * 64:(r + 1) * 64, pr, :],
                                                              rhs=kiT2[r * 64:(r + 1) * 64, p0:p0 + n], start=True, stop=True),
                                     reads=[('qiT_t', b)] + [('kiT', kt) for kt in range(p0 // P, (p0 + n) // P)], writes=[('pS', ri % 2)])
                                S.op('act', lambda e: e.activation(out=rl[ri % 2][:, 0:n], in_=pb[:, 0:n], func=AF.Relu),
                                     reads=[('pS', ri % 2)], writes=[('rl', ri % 2)])
                                if h == 0:
                                    S.op('dve', lambda e: e.tensor_scalar(out=sc[:, p0:p0 + n], in0=rl[ri % 2][:, 0:n], scalar1=ws_t[:, 0:1], scalar2=None, op0=ALU.mult),
                                         reads=[('rl', ri % 2), 'ws_t'], writes=['sc'])
                                else:
                                    S.op('dve', lambda e: e.scalar_tensor_tensor(out=sc[:, p0:p0 + n], in0=rl[ri % 2][:, 0:n], scalar=ws_t[:, h:h + 1],
                                                                                 in1=sc[:, p0:p0 + n], op0=ALU.mult, op1=ALU.add),
                                         reads=[('rl', ri % 2), 'ws_t', 'sc'], writes=['sc'])
                                ri += 1
                        S.op('dve', lambda e: e.tensor_reduce(out=bst[:, 0:1], in_=sc[:, 0:N], axis=AX.X, op=ALU.max, apply_absolute_value=True),
                             reads=['sc'], writes=['bst'])
                        S.op('pool', lambda e: e.memset(sc[0:64, N - 64:N], NEG), reads=['bst'], writes=['sc'])
                        S.op('dve', lambda e: e.tensor_scalar(out=Bk[:], in0=pw[:], scalar1=bst[:, 0:1], scalar2=None, op0=ALU.mult),
                             reads=['bst', 'pw'], writes=['Bk'])
                        S.op('dve', lambda e: e.memset(bst[:, 1:2], 0.0), reads=['bst'], writes=['bst'])
                        for k in range(NBIS + 1):
                            S.op('dve', lambda e: e.tensor_scalar(out=Mk[:, 0:N], in0=sc[:, 0:N], scalar1=bst[:, 1:2], scalar2=None,
                                                                  op0=ALU.is_ge, op1=ALU.add, accum_out=bst[:, 2:3]),
                                 reads=['sc', 'bst'], writes=['Mk', 'bst'])
                            if k < NBIS:
                                S.op('dve', lambda e: e.tensor_scalar(out=bst[:, 3:4], in0=bst[:, 2:3], scalar1=TOPK - 0.5, scalar2=-0.5,
                                                                      op0=ALU.is_ge, op1=ALU.add), reads=['bst'], writes=['bst'])
                                S.op('dve', lambda e, k=k: e.scalar_tensor_tensor(out=bst[:, 1:2], in0=bst[:, 3:4], scalar=Bk[:, k:k + 1],
                                                                                  in1=bst[:, 1:2], op0=ALU.mult, op1=ALU.add),
                                     reads=['bst', 'Bk'], writes=['bst'])
                            else:
                                S.op('dve', lambda e: e.tensor_scalar(out=bst[:, 3:4], in0=bst[:, 2:3], scalar1=TOPK - 0.5, scalar2=-1.0,
                                                                      op0=ALU.is_ge, op1=ALU.add), reads=['bst'], writes=['bst'])
                                S.op('dve', lambda e: e.scalar_tensor_tensor(out=bst[:, 4:5], in0=bst[:, 3:4], scalar=Bk[:, NBIS:NBIS + 1],
                                                                             in1=bst[:, 1:2], op0=ALU.mult, op1=ALU.add),
                                     reads=['bst', 'Bk'], writes=['bst'])
                        S.op('dve', lambda e: e.tensor_scalar(out=Mk[:, 0:N], in0=sc[:, 0:N], scalar1=bst[:, 4:5], scalar2=None, op0=ALU.is_ge),
                             reads=['sc', 'bst'], writes=['Mk'])
                        for r0 in range(0, j + 1, 8):
                            nb = min(8, j + 1 - r0)
                            pT, pk = next_psT()
                            for i in range(nb):
                                S.op('pe', lambda e, i=i: e.transpose(out=pT[:, i * P:(i + 1) * P], in_=Mk[:, (r0 + i) * P:(r0 + i + 1) * P], identity=ident[:]),
                                     reads=['Mk', 'ident'], writes=[pk], inc=(i == nb - 1))
                            S.op('act', lambda e: e.activation(out=MT[:, r0 * P:(r0 + nb) * P], in_=pT[:, 0:nb * P], func=AF.Copy),
                                 reads=[pk], writes=['MT'])
                        chk('A_idx%d' % t)
                        for h in range(6):
                            pr, r = divmod(h, 2)
                            for g0 in range(0, j + 1, 4):
                                kts = list(range(g0, min(g0 + 4, j + 1)))
                                n = len(kts) * P
                                pb = pS[ri % 2]
                                for i, kt in enumerate(kts):
                                    S.op('pe', lambda e, i=i, kt=kt: e.matmul(pb[:, i * P:(i + 1) * P], lhsT=kT[r * 64:(r + 1) * 64, pr, kt * P:(kt + 1) * P],
                                                                              rhs=qT_t[b][r * 64:(r + 1) * 64, pr, :], start=True, stop=True),
                                         reads=[('kT', kt), ('qT_t', b)], writes=[('pS', ri % 2)], inc=(i == len(kts) - 1))
                                S.op('act', lambda e: e.activation(out=Eb[gi % 2][:, 0:n], in_=pb[:, 0:n], func=AF.Exp, scale=0.125),
                                     reads=[('pS', ri % 2)], writes=[('Eb', gi % 2)])
                                S.op('dve', lambda e: e.tensor_tensor(out=PT[gi % 2][:, 0:n], in0=Eb[gi % 2][:, 0:n], in1=MT[:, g0 * P:g0 * P + n], op=ALU.mult),
                                     reads=[('Eb', gi % 2), 'MT'], writes=[('PT', gi % 2)])
                                for i, kt in enumerate(kts):
                                    S.op('pe', lambda e, i=i, kt=kt: e.matmul(pO[:, h * 65:(h + 1) * 65], lhsT=PT[gi % 2][:, i * P:(i + 1) * P],
                                                                              rhs=v_aug[:, kt, h, :], start=(kt == 0), stop=(kt == j)),
                                         reads=[('PT', gi % 2), ('v', kt)], writes=['pO'], inc=(i == len(kts) - 1))
                                gi += 1
                                ri += 1
                        normalize_cat(6, catb, rden)
                        out_proj(j, catb, catT, 3, Wo, 0)
                        chk('A_att%d' % t)
                    S.barrier()
                chk('A')

                load_w(Wo[:, 0:3, :], wov[:, 3:6, :], 'Wo')
                load_w(W1[:, :, 0:256], wv[:, :, 2628:2884], 'W1')
                load_w(W1[:, :, 256:768], w_view(w_mkv_d, l)[:, :, 0:512], 'W1')
                with ExitStack() as sg:
                    expb = sb("expb", [P, 6, 5, P], F32, sg)
                    Ef = [sb("Ef%d" % i, [P, 512], F32, sg) for i in range(2)]
                    S.dma('sp', out=expb[:].rearrange("p h m q -> p (h m q)"), in_=bias_d[l].rearrange("p h m q -> p (h m q)"), writes=['expb'])
                    S.op('act', lambda e: e.activation(out=expb[:].rearrange("p h m q -> p (h m q)"), in_=expb[:].rearrange("p h m q -> p (h m q)"), func=AF.Exp),
                         reads=['expb'], writes=['expb'])
                    S.op('pool', lambda e: e.memset(expb[64:128, :, 0, 0:64], 0.0), reads=['expb'], writes=['expb'])
                    S.op('pool', lambda e: e.memset(expb[0:64, :, 4, 64:128], 0.0), reads=['expb'], writes=['expb'])
                    gi = 0
                    ri = 0
                    for t in range(NT):
                        b = t % 2
                        make_hT(t, hTt[b][:], ('hTt', b))
                        for s_, (c0, c1) in enumerate(((0, 512), (512, 1024), (1024, 1152))):
                            pb = pP[s_ % 2]
                            for c in range(DC):
                                S.op('pe', lambda e, c=c: e.matmul(pb[:, 0:c1 - c0], lhsT=hTt[b][:, c, :], rhs=W2[:, c, c0:c1],
                                                                   start=(c == 0), stop=(c == DC - 1)),
                                     reads=[('hTt', b), 'W2'], writes=[('pP', s_ % 2)], inc=(c == DC - 1))
                            S.op('act', lambda e: e.activation(out=pj[:, c0:c1], in_=pb[:, 0:c1 - c0], func=AF.Copy),
                                 reads=[('pP', s_ % 2)], writes=['pj'])
                        S.op('act', lambda e: e.activation(out=sq[:, 0:768], in_=pj[:, 0:768], func=AF.Square), reads=['pj'], writes=['sq'])
                        head_rstd(sq[:, 0:768], 12, ssh, rsh)
                        S.op('dve', lambda e: e.tensor_tensor(out=qn[:, 0:768].rearrange("p (h d) -> p h d", d=64),
                                                              in0=pj[:, 0:768].rearrange("p (h d) -> p h d", d=64),
                                                              in1=rsh[:, 0:12].unsqueeze(2).to_broadcast([P, 12, 64]), op=ALU.mult),
                             reads=['pj', 'rsh'], writes=['qn'])
                        S.op('dve', lambda e: e.tensor_tensor(out=qkb[:, 0:384].rearrange("p (h d) -> p h d", d=64),
                                                              in0=qn[:, 0:384].rearrange("p (h d) -> p h d", d=64),
                                                              in1=hgl[:, 3, :].unsqueeze(1).to_broadcast([P, 6, 64]), op=ALU.mult),
                             reads=['qn', hk], writes=['qkb'])
                        S.op('dve', lambda e: e.tensor_tensor(out=qkb[:, 384:768].rearrange("p (h d) -> p h d", d=64),
                                                              in0=qn[:, 384:768].rearrange("p (h d) -> p h d", d=64),
                                                              in1=hgl[:, 4, :].unsqueeze(1).to_broadcast([P, 6, 64]), op=ALU.mult),
                             reads=['qn', hk], writes=['qkb'])
                        pT, pk = next_psT()
                        for i in range(6):
                            S.op('pe', lambda e, i=i: e.transpose(out=pT[:, i * P:(i + 1) * P], in_=qkb[:, i * P:(i + 1) * P], identity=ident[:]),
                                 reads=['qkb', 'ident'], writes=[pk], inc=(i == 5))
                        S.op('dve', lambda e: e.tensor_copy(out=qT_t[b][:].rearrange("p c k -> p (c k)"), in_=pT[:, 0:384]),
                             reads=[pk], writes=[('qT_t', b)])
                        S.op('act', lambda e: e.activation(out=kT[:, :, t * P:(t + 1) * P], in_=pT[:, 384:768].rearrange("p (c k) -> p c k", c=3), func=AF.Copy),
                             reads=[pk], writes=[('kT', t)])
                        S.op('pool', lambda e: e.tensor_copy(out=v_aug[:, t, :, 0:64], in_=pj[:, 768:1152].rearrange("p (h d) -> p h d", d=64)),
                             reads=['pj', 'v_all'], writes=[('v', t)])
                        j = t
                        ms = [m for m in range(5) if j - m >= 0]
                        for h in range(6):
                            pr, r = divmod(h, 2)
                            for chunk in (ms[0:4], ms[4:5]):
                                if not chunk:
                                    continue
                                n = len(chunk) * P
                                pb = pS[ri % 2]
                                for i, m in enumerate(chunk):
                                    kt = j - m
                                    S.op('pe', lambda e, i=i, kt=kt: e.matmul(pb[:, i * P:(i + 1) * P], lhsT=kT[r * 64:(r + 1) * 64, pr, kt * P:(kt + 1) * P],
                                                                              rhs=qT_t[b][r * 64:(r + 1) * 64, pr, :], start=True, stop=True),
                                         reads=[('kT', kt), ('qT_t', b)], writes=[('pS', ri % 2)], inc=(i == len(chunk) - 1))
                                S.op('act', lambda e: e.activation(out=Ef[gi % 2][:, 0:n], in_=pb[:, 0:n], func=AF.Exp, scale=0.125),
                                     reads=[('pS', ri % 2)], writes=[('Ef', gi % 2)])
                                m0 = chunk[0]
                                S.op('dve', lambda e: e.tensor_tensor(out=PT[gi % 2][:, 0:n].rearrange("p (m q) -> p m q", q=P),
                                                                      in0=Ef[gi % 2][:, 0:n].rearrange("p (m q) -> p m q", q=P),
                                                                      in1=expb[:, h, m0:m0 + len(chunk), :], op=ALU.mult),
                                     reads=[('Ef', gi % 2), 'expb'], writes=[('PT', gi % 2)])
                                for i, m in enumerate(chunk):
                                    kt = j - m
                                    S.op('pe', lambda e, i=i, kt=kt, m=m: e.matmul(pO[:, h * 65:(h + 1) * 65], lhsT=PT[gi % 2][:, i * P:(i + 1) * P],
                                                                                   rhs=v_aug[:, kt, h, :], start=(m == ms[0]), stop=(m == ms[-1])),
                                         reads=[('PT', gi % 2), ('v', kt)], writes=['pO'], inc=(i == len(chunk) - 1))
                                gi += 1
                                ri += 1
                        normalize_cat(6, catb, rden)
                        out_proj(j, catb, catT, 3, Wo, 0)
                    S.barrier()

                chk('B')
                with ExitStack() as sg:
                    mem_sb = sb("mem_sb", [P, 2, D], F32, sg)
                    memT = sb("memT", [P, DC, 256], BF16, sg)
                    mkT = sb("mkT", [P, 2, 256], BF16, sg)
                    mv_aug = sb("mv_aug", [P, 2, 4, 65], BF16, sg)
                    mkv = sb("mkv", [P, 512], F32, sg)
                    load_w(Wo[:, 0:2, :], wov[:, 6:8, :], 'Wo')
                    load_gbc(g_mem_d[l:l + 1, :])
                    S.op('pool', lambda e: e.memset(mv_aug[:], 1.0), writes=['mv_aug'])
                    for mt in range(2):
                        S.dma('sp', out=mem_sb[:, mt, :], in_=mem_d[mt * P:(mt + 1) * P, :], writes=[('mem', mt)])
                    for mt in range(2):
                        b = mt % 2
                        S.op('dve', lambda e: e.scalar_tensor_tensor(out=hb[b][:], in0=mem_sb[:, mt, :], scalar=rstd_mem[:, mt:mt + 1],
                                                                     in1=gbc[:], op0=ALU.mult, op1=ALU.mult),
                             reads=[('mem', mt), 'rstd_mem', 'gbc'], writes=[('hb', b)])
                        pT, pk = next_psT()
                        for c in range(DC):
                            S.op('pe', lambda e, c=c: e.transpose(out=pT[:, c * P:(c + 1) * P], in_=hb[b][:, c * P:(c + 1) * P], identity=ident[:]),
                                 reads=[('hb', b), 'ident'], writes=[pk], inc=(c == DC - 1))
                        S.op('act', lambda e: e.activation(out=memT[:, :, mt * P:(mt + 1) * P], in_=pT[:].rearrange("p (c k) -> p c k", c=DC), func=AF.Copy),
                             reads=[pk], writes=['memT'])
                    for mt in range(2):
                        pb = pP[mt % 2]
                        for c in range(DC):
                            S.op('pe', lambda e, c=c: e.matmul(pb[:, 0:512], lhsT=memT[:, c, mt * P:(mt + 1) * P], rhs=W1[:, c, 256:768],
                                                               start=(c == 0), stop=(c == DC - 1)),
                                 reads=['memT', 'W1'], writes=[('pP', mt % 2)], inc=(c == DC - 1))
                        S.op('act', lambda e: e.activation(out=mkv[:], in_=pb[:, 0:512], func=AF.Copy), reads=[('pP', mt % 2)], writes=['mkv'])
                        S.op('act', lambda e: e.activation(out=sq[:, 0:256], in_=mkv[:, 0:256], func=AF.Square), reads=['mkv'], writes=['sq'])
                        head_rstd(sq[:, 0:256], 4, ssh, rsh)
                        S.op('dve', lambda e: e.tensor_tensor(out=qn[:, 0:256].rearrange("p (h d) -> p h d", d=64),
                                                              in0=mkv[:, 0:256].rearrange("p (h d) -> p h d", d=64),
                                                              in1=rsh[:, 0:4].unsqueeze(2).to_broadcast([P, 4, 64]), op=ALU.mult),
                             reads=['mkv', 'rsh'], writes=['qn'])
                        S.op('dve', lambda e: e.tensor_tensor(out=qkb[:, 0:256].rearrange("p (h d) -> p h d", d=64),
                                                              in0=qn[:, 0:256].rearrange("p (h d) -> p h d", d=64),
                                                              in1=hgl[:, 6, :].unsqueeze(1).to_broadcast([P, 4, 64]), op=ALU.mult),
                             reads=['qn', hk], writes=['qkb'])
                        pT, pk = next_psT()
                        for i in range(2):
                            S.op('pe', lambda e, i=i: e.transpose(out=pT[:, i * P:(i + 1) * P], in_=qkb[:, i * P:(i + 1) * P], identity=ident[:]),
                                 reads=['qkb', 'ident'], writes=[pk], inc=(i == 1))
                        S.op('dve', lambda e: e.tensor_copy(out=mkT[:, :, mt * P:(mt + 1) * P], in_=pT[:, 0:256].rearrange("p (c k) -> p c k", c=2)),
                             reads=[pk], writes=['mkT'])
                        S.op('pool', lambda e: e.tensor_copy(out=mv_aug[:, mt, :, 0:64], in_=mkv[:, 256:512].rearrange("p (h d) -> p h d", d=64)),
                             reads=['mkv', 'mv_aug'], writes=['mv_aug'])
                    chk('M_mem')
                    load_gbc(g_mix_d[l:l + 1, :])
                    gi = 0
                    ri = 0
                    for t in range(NT):
                        if t > 0:
                            chk('M_t%d' % (t - 1))
                        b = t % 2
                        make_hT(t, hTt[b][:], ('hTt', b))
                        pb = pP[t % 2]
                        for c in range(DC):
                            S.op('pe', lambda e, c=c: e.matmul(pb[:, 0:256], lhsT=hTt[b][:, c, :], rhs=W1[:, c, 0:256],
                                                               start=(c == 0), stop=(c == DC - 1)),
                                 reads=[('hTt', b), 'W1'], writes=[('pP', t % 2)], inc=(c == DC - 1))
                        S.op('act', lambda e: e.activation(out=pj[:, 0:256], in_=pb[:, 0:256], func=AF.Copy), reads=[('pP', t % 2)], writes=['pj'])
                        S.op('act', lambda e: e.activation(out=sq[:, 0:256], in_=pj[:, 0:256], func=AF.Square), reads=['pj'], writes=['sq'])
                        head_rstd(sq[:, 0:256], 4, ssh, rsh)
                        S.op('dve', lambda e: e.tensor_tensor(out=qn[:, 0:256].rearrange("p (h d) -> p h d", d=64),
                                                              in0=pj[:, 0:256].rearrange("p (h d) -> p h d", d=64),
                                                              in1=rsh[:, 0:4].unsqueeze(2).to_broadcast([P, 4, 64]), op=ALU.mult),
                             reads=['pj', 'rsh'], writes=['qn'])
                        S.op('dve', lambda e: e.tensor_tensor(out=qkb[:, 0:256].rearrange("p (h d) -> p h d", d=64),
                                                              in0=qn[:, 0:256].rearrange("p (h d) -> p h d", d=64),
                                                              in1=hgl[:, 5, :].unsqueeze(1).to_broadcast([P, 4, 64]), op=ALU.mult),
                             reads=['qn', hk], writes=['qkb'])
                        pT, pk = next_psT()
                        for i in range(2):
                            S.op('pe', lambda e, i=i: e.transpose(out=pT[:, i * P:(i + 1) * P], in_=qkb[:, i * P:(i + 1) * P], identity=ident[:]),
                                 reads=['qkb', 'ident'], writes=[pk], inc=(i == 1))
                        S.op('dve', lambda e: e.tensor_copy(out=qT_t[b][:, 0:2, :].rearrange("p c k -> p (c k)"), in_=pT[:, 0:256]),
                             reads=[pk], writes=[('qT_t', b)])
                        chk('M_q%d' % t)
                        for h2 in range(2):
                            pb2 = pS[ri % 2]
                            blocks = [(h2 + 2 * hh, kt) for hh in range(2) for kt in range(2)]
                            for i, (h, kt) in enumerate(blocks):
                                pr, r = divmod(h, 2)
                                S.op('pe', lambda e, i=i, kt=kt, pr=pr, r=r: e.matmul(pb2[:, i * P:(i + 1) * P], lhsT=mkT[r * 64:(r + 1) * 64, pr, kt * P:(kt + 1) * P],
                                                                                      rhs=qT_t[b][r * 64:(r + 1) * 64, pr, :], start=True, stop=True),
                                     reads=['mkT', ('qT_t', b)], writes=[('pS', ri % 2)], inc=(i == 3))
                            S.op('act', lambda e: e.activation(out=Eb[gi % 2][:, :], in_=pb2[:, :], func=AF.Exp, scale=0.125),
                                 reads=[('pS', ri % 2)], writes=[('Eb', gi % 2)])
                            chk('M_s%d_%d' % (t, h2))
                            for i, (h, kt) in enumerate(blocks):
                                S.op('pe', lambda e, i=i, kt=kt, h=h: e.matmul(pO[:, h * 65:(h + 1) * 65], lhsT=Eb[gi % 2][:, i * P:(i + 1) * P],
                                                                               rhs=mv_aug[:, kt, h, :], start=(kt == 0), stop=(kt == 1)),
                                     reads=[('Eb', gi % 2), 'mv_aug'], writes=['pO'], inc=(i == 3))
                            chk('M_p%d_%d' % (t, h2))
                            gi += 1
                            ri += 1
                        chk('M_v%d' % t)
                        normalize_cat(4, catb, rden)
                        chk('M_n%d' % t)
                        out_proj(t, catb, catT, 2, Wo, 0)
                    S.barrier()

            chk('M')
            with ExitStack() as sf:
                hTg = sb("hTg", [P, DC, 512], BF16, sf)
                actT = sb("actT", [P, FB, 512], BF16, sf)
                Wd = sb("Wd", [P, FB, D], BF16, sf)
                Wg = [sb("Wg%d" % i, [P, DC, 512], BF16, sf) for i in range(2)]
                Wu = [sb("Wu%d" % i, [P, DC, 512], BF16, sf) for i in range(2)]
                sg_ = [sb("sg%d" % i, [P, 512], F32, sf) for i in range(2)]
                load_gbc(g_ffn_d[l:l + 1, :])
                norm_stats()
                wgu = w_view(w_gu_d, l)
                wdv = w_view(w_dn_d, l)
                si = 0
                fi = 0
                chk('F_n')
                for G in range(4):
                    for tt in range(4):
                        t = 4 * G + tt
                        make_hT(t, hTg[:, :, tt * P:(tt + 1) * P], 'hTg')
                    chk('F_h%d' % G)
                    for s_ in range(6):
                        f0 = s_ * 512
                        nf = min(512, DFF - f0)
                        wb = si % 2
                        load_w(Wg[wb][:, :, 0:nf], wgu[:, :, f0:f0 + nf], ('Wg', wb))
                        load_w(Wu[wb][:, :, 0:nf], wgu[:, :, DFF + f0:DFF + f0 + nf], ('Wu', wb))
                        if G == 0 and s_ == 1:
                            for q4 in range(4):
                                load_w(Wd[:, :, q4 * 256:(q4 + 1) * 256], wdv[:, :, q4 * 256:(q4 + 1) * 256], 'Wd')
                        for i in range(nf // P):
                            fb = (f0 // P) + i
                            pg = pP[fi % 2]
                            pu = pS[fi % 2]
                            for c in range(DC):
                                S.op('pe', lambda e, c=c: e.matmul(pg[:, :], lhsT=Wg[wb][:, c, i * P:(i + 1) * P], rhs=hTg[:, c, :],
                                                                   start=(c == 0), stop=(c == DC - 1)),
                                     reads=[('Wg', wb), 'hTg'], writes=[('pP', fi % 2)], inc=(c == DC - 1))
                            for c in range(DC):
                                S.op('pe', lambda e, c=c: e.matmul(pu[:, :], lhsT=Wu[wb][:, c, i * P:(i + 1) * P], rhs=hTg[:, c, :],
                                                                   start=(c == 0), stop=(c == DC - 1)),
                                     reads=[('Wu', wb), 'hTg'], writes=[('pS', fi % 2)], inc=(c == DC - 1))
                            S.op('act', lambda e: e.activation(out=sg_[fi % 2][:], in_=pg[:, :], func=AF.Silu),
                                 reads=[('pP', fi % 2)], writes=[('sg', fi % 2)])
                            S.op('dve', lambda e: e.tensor_tensor(out=actT[:, fb, :], in0=sg_[fi % 2][:], in1=pu[:, :], op=ALU.mult),
                                 reads=[('sg', fi % 2), ('pS', fi % 2)], writes=['actT'])
                            fi += 1
                        si += 1
                        chk('F_g%d_%d' % (G, s_))
                    for tt in range(4):
                        t = 4 * G + tt
                        for half in range(2):
                            pb = pO if half == 0 else pY
                            pkey = 'pO' if half == 0 else 'pY'
                            for fb in range(FB):
                                S.op('pe', lambda e, fb=fb: e.matmul(pb[:, :], lhsT=actT[:, fb, tt * P:(tt + 1) * P],
                                                                     rhs=Wd[:, fb, half * 512:(half + 1) * 512],
                                                                     start=(fb == 0), stop=(fb == FB - 1)),
                                     reads=['actT', 'Wd'], writes=[pkey], inc=(fb == FB - 1))
                            S.op('dve', lambda e: e.tensor_tensor(out=x_sb[:, t, half * 512:(half + 1) * 512],
                                                                  in0=x_sb[:, t, half * 512:(half + 1) * 512], in1=pb[:, :], op=ALU.add),
                                 reads=[pkey, ('x', t)], writes=[('x', t)])
                S.barrier()

        finish()
    return nc


def _host_layout(inputs):
    f = lambda a: np.ascontiguousarray(np.asarray(a))
    x = f(inputs["x"]); mem = f(inputs["mem"]); pos = f(inputs["positions"]).astype(np.int32)
    hgains = np.stack([f(inputs[k]) for k in ("g_q_a", "g_k_a", "g_k_idx", "g_q_b", "g_k_b", "g_q_m", "g_k_m")], axis=1)
    rb = f(inputs["rel_bias"])
    k = np.arange(P)[:, None, None]
    m = np.arange(5)[None, :, None]
    q = np.arange(P)[None, None, :]
    idx = np.clip(128 * m + q - k, -256, 256) + 256
    biasT = rb[:, :, idx]
    biasT = np.ascontiguousarray(biasT.transpose(0, 2, 1, 3, 4)).astype(np.float32)
    shared = {
        "g_mix": f(inputs["g_mix"]), "g_ffn": f(inputs["g_ffn"]), "g_mem": f(inputs["g_mem"]),
        "hgains": np.ascontiguousarray(hgains.astype(np.float32)),
        "w_in": f(inputs["w_in"]), "w_mem_kv": f(inputs["w_mem_kv"]), "w_out": f(inputs["w_out"]),
        "w_gate_up": f(inputs["w_gate_up"]), "w_down": f(inputs["w_down"]), "biasT": biasT,
    }
    in_maps = []
    for b in range(8):
        d = dict(shared)
        d["x"] = x[b]
        d["mem"] = mem[b]
        d["pos"] = np.ascontiguousarray(pos[b].reshape(NT, P).T)
        in_maps.append(d)
    return in_maps


def build_safe(depth=DEPTH, stop=None):
    try:
        return build(depth, stop)
    except _Stop:
        return _Stop.nc


def kernel(**inputs):
    depth = int(os.environ.get("KDEPTH", DEPTH))
    stop = os.environ.get("KSTOP") or None
    ncores = int(os.environ.get("KCORES", 8))
    nc = build_safe(depth, stop)
    in_maps = _host_layout(inputs)[:ncores]
    res = run_bass_kernel_spmd(nc, in_maps, core_ids=list(range(ncores)))
    out = np.stack([np.asarray(r["out"]) for r in res.results], axis=0)
    return out.astype(np.float32)
```
